# Optimizing a Trainium2 kernel written in Bass

```python
import jax, jax.numpy as jnp
from jax import lax
import numpy as np

D_MODEL = 1024
BATCH = 8
SEQ = 2048
DEPTH = 2
DEC_BATCH = 128
DEC_SEQ = 4
PAST_LEN = 16384
PAGE_SIZE = 128

N_MIXERS = 2
N_GDN_LAYERS = (DEPTH + 1) // 2
N_SSM_LAYERS = DEPTH // 2
CONV_WIDTH = 4
EPS = 1e-6

GDN_QK_HEADS = 8
GDN_V_HEADS = 16
GDN_HEAD_DIM = 128
GDN_QK_DIM = GDN_QK_HEADS * GDN_HEAD_DIM
GDN_V_DIM = GDN_V_HEADS * GDN_HEAD_DIM
GDN_CONV_DIM = 2 * GDN_QK_DIM + GDN_V_DIM
GDN_PROJ = GDN_CONV_DIM + GDN_V_DIM + 2 * GDN_V_HEADS
GDN_CHUNK = 64

SSM_D_INNER = 2 * D_MODEL
SSM_HEAD_DIM = 64
SSM_HEADS = SSM_D_INNER // SSM_HEAD_DIM
SSM_GROUPS = 4
SSM_STATE = 128
SSM_CONV_DIM = SSM_D_INNER + 2 * SSM_GROUPS * SSM_STATE
SSM_PROJ = SSM_D_INNER + SSM_CONV_DIM + SSM_HEADS
SSM_CHUNK = 64

FFN_HIDDEN = ((-(-8 * D_MODEL // 3) + 255) // 256) * 256

kernel_name = "hybrid_gdn_ssd_adaln_decode_step"


def _rmsnorm(x, w):
    xf = x.astype(jnp.float32)
    y = xf * lax.rsqrt(jnp.mean(xf * xf, axis=-1, keepdims=True) + EPS)
    return (y * w.astype(jnp.float32)).astype(x.dtype)


def _l2norm(x):
    return x * lax.rsqrt(jnp.sum(x * x, axis=-1, keepdims=True) + EPS)


def _causal_conv(u, buf, w, bias=None):
    L = u.shape[1]
    ext = jnp.concatenate([buf.astype(u.dtype), u], axis=1)
    out = sum(ext[:, t:t + L] * w[t] for t in range(CONV_WIDTH))
    if bias is not None:
        out = out + bias
    return out, ext[:, L:].astype(buf.dtype)


def _chunk(t, c, n):
    pad = n * c - t.shape[1]
    t = jnp.pad(t, [(0, 0), (0, pad)] + [(0, 0)] * (t.ndim - 2))
    return t.reshape((t.shape[0], n, c) + t.shape[2:])


def _gated_delta_rule(q, k, v, g, beta, s0):
    bsz, L, H, _ = q.shape
    V = v.shape[-1]
    c = min(GDN_CHUNK, L)
    n = -(-L // c)
    blk = lambda t: jnp.swapaxes(_chunk(t, c, n), 2, 3)
    q, k, v, g, beta = blk(q), blk(k), blk(v), blk(g), blk(beta)
    cum = jnp.cumsum(g, axis=-1)
    causal = jnp.tril(jnp.ones((c, c), dtype=bool))
    strict = jnp.tril(jnp.ones((c, c), dtype=bool), k=-1)
    seg = cum[..., :, None] - cum[..., None, :]
    decay = jnp.exp(jnp.where(causal, seg, -jnp.inf))
    kb = k * beta[..., None]
    kk = jnp.einsum('bnhik,bnhjk->bnhij', kb, k) * decay
    tri = jnp.eye(c, dtype=jnp.float32) + jnp.where(strict, kk, 0.0)
    rhs = jnp.concatenate([v * beta[..., None], kb * jnp.exp(cum)[..., None]], axis=-1)
    sol = lax.linalg.triangular_solve(tri, rhs, left_side=True, lower=True, unit_diagonal=True)
    u, w = sol[..., :V], sol[..., V:]
    qk = jnp.einsum('bnhik,bnhjk->bnhij', q, k) * decay
    q_dec = q * jnp.exp(cum)[..., None]
    last = cum[..., -1]
    k_dec = k * jnp.exp(last[..., None] - cum)[..., None]

    def step(S, xs):
        u_, w_, qk_, qd_, kd_, l_ = xs
        delta = u_ - jnp.einsum('bhik,bhkv->bhiv', w_, S)
        o = jnp.einsum('bhik,bhkv->bhiv', qd_, S) + jnp.einsum('bhij,bhjv->bhiv', qk_, delta)
        S = jnp.exp(l_)[..., None, None] * S + jnp.einsum('bhjk,bhjv->bhkv', kd_, delta)
        return S, o

    xs = tuple(jnp.moveaxis(t, 1, 0) for t in (u, w, qk, q_dec, k_dec, last))
    S, o = lax.scan(step, s0.astype(jnp.float32), xs)
    o = jnp.swapaxes(jnp.moveaxis(o, 0, 1), 2, 3).reshape(bsz, n * c, H, V)[:, :L]
    return o, S


def _ssd_scan(x, dt, a, bm, cm, h0):
    bsz, L, H, P = x.shape
    G, N = bm.shape[2], bm.shape[3]
    R = H // G
    c = min(SSM_CHUNK, L)
    n = -(-L // c)
    xdt = _chunk(x * dt[..., None], c, n).reshape(bsz, n, c, G, R, P)
    la = _chunk(dt * a, c, n).reshape(bsz, n, c, G, R)
    bm, cm = _chunk(bm, c, n), _chunk(cm, c, n)
    cum = jnp.cumsum(la, axis=2)
    causal = jnp.tril(jnp.ones((c, c), dtype=bool))[:, :, None, None]
    seg = cum[:, :, :, None] - cum[:, :, None]
    decay = jnp.exp(jnp.where(causal, seg, -jnp.inf))
    cb = jnp.einsum('bnigs,bnjgs->bnijg', cm, bm)
    y_diag = jnp.einsum('bnijg,bnijgr,bnjgrp->bnigrp', cb, decay, xdt)
    last = cum[:, :, -1]
    to_end = jnp.exp(last[:, :, None] - cum)
    local = jnp.einsum('bnjgr,bnjgrp,bnjgs->bngrps', to_end, xdt, bm)

    def step(h, xs):
        l_, loc = xs
        return jnp.exp(l_)[..., None, None] * h + loc, h

    h_init = h0.astype(jnp.float32).reshape(bsz, G, R, P, N)
    h_fin, h_prev = lax.scan(step, h_init, (jnp.moveaxis(last, 1, 0), jnp.moveaxis(local, 1, 0)))
    h_prev = jnp.moveaxis(h_prev, 0, 1)
    y_off = jnp.einsum('bnigs,bngrps,bnigr->bnigrp', cm, h_prev, jnp.exp(cum))
    y = (y_diag + y_off).reshape(bsz, n * c, H, P)[:, :L]
    return y, h_fin.reshape(bsz, H, P, N)


def _gdn_mixer(h, s0, buf, w_in, conv_w, a_log, dt_bias, norm_w, w_out):
    bsz, L, _ = h.shape
    proj = h @ w_in
    qkv, z, beta_raw, a_raw = jnp.split(
        proj, [GDN_CONV_DIM, GDN_CONV_DIM + GDN_V_DIM, GDN_CONV_DIM + GDN_V_DIM + GDN_V_HEADS], axis=-1)
    qkv, new_buf = _causal_conv(qkv, buf, conv_w)
    qkv = jax.nn.silu(qkv.astype(jnp.float32))
    q, k, v = jnp.split(qkv, [GDN_QK_DIM, 2 * GDN_QK_DIM], axis=-1)
    rep = GDN_V_HEADS // GDN_QK_HEADS
    q = jnp.repeat(_l2norm(q.reshape(bsz, L, GDN_QK_HEADS, GDN_HEAD_DIM)), rep, axis=2) * GDN_HEAD_DIM ** -0.5
    k = jnp.repeat(_l2norm(k.reshape(bsz, L, GDN_QK_HEADS, GDN_HEAD_DIM)), rep, axis=2)
    v = v.reshape(bsz, L, GDN_V_HEADS, GDN_HEAD_DIM)
    beta = jax.nn.sigmoid(beta_raw.astype(jnp.float32))
    g = -jnp.exp(a_log.astype(jnp.float32)) * jax.nn.softplus(a_raw.astype(jnp.float32) + dt_bias.astype(jnp.float32))
    o, s_new = _gated_delta_rule(q, k, v, g, beta, s0)
    o = _rmsnorm(o, norm_w) * jax.nn.silu(z.astype(jnp.float32).reshape(bsz, L, GDN_V_HEADS, GDN_HEAD_DIM))
    out = o.reshape(bsz, L, GDN_V_DIM).astype(h.dtype) @ w_out
    return out, s_new.astype(s0.dtype), new_buf


def _ssm_mixer(h, s0, buf, w_in, conv_w, conv_b, a_log, dt_bias, d_skip, norm_w, w_out):
    bsz, L, _ = h.shape
    proj = h @ w_in
    z, xbc, dt_raw = jnp.split(proj, [SSM_D_INNER, SSM_D_INNER + SSM_CONV_DIM], axis=-1)
    xbc, new_buf = _causal_conv(xbc, buf, conv_w, conv_b)
    xbc = jax.nn.silu(xbc.astype(jnp.float32))
    xs, bm, cm = jnp.split(xbc, [SSM_D_INNER, SSM_D_INNER + SSM_GROUPS * SSM_STATE], axis=-1)
    xs = xs.reshape(bsz, L, SSM_HEADS, SSM_HEAD_DIM)
    bm = bm.reshape(bsz, L, SSM_GROUPS, SSM_STATE)
    cm = cm.reshape(bsz, L, SSM_GROUPS, SSM_STATE)
    dt = jax.nn.softplus(dt_raw.astype(jnp.float32) + dt_bias.astype(jnp.float32))
    a = -jnp.exp(a_log.astype(jnp.float32))
    y, s_new = _ssd_scan(xs, dt, a, bm, cm, s0)
    y = y + d_skip.astype(jnp.float32)[:, None] * xs
    y = y.reshape(bsz, L, SSM_D_INNER) * jax.nn.silu(z.astype(jnp.float32))
    gsz = SSM_D_INNER // SSM_GROUPS
    y = _rmsnorm(y.reshape(bsz, L, SSM_GROUPS, gsz), norm_w.reshape(SSM_GROUPS, gsz)).reshape(bsz, L, SSM_D_INNER)
    out = y.astype(h.dtype) @ w_out
    return out, s_new.astype(s0.dtype), new_buf


def _swiglu(h, w_gate_up, w_down):
    gate, up = jnp.split(h @ w_gate_up, 2, axis=-1)
    return (jax.nn.silu(gate) * up) @ w_down


def _trunk(x, c, st_gdn, cv_gdn, st_ssm, cv_ssm, p):
    new_gdn_s, new_gdn_c, new_ssm_s, new_ssm_c = [], [], [], []
    cs = jax.nn.silu(c)
    for i in range(DEPTH):
        mod = cs @ p['w_mod'][i] + p['b_mod'][i]
        sh1, sc1, gt1, sh2, sc2, gt2 = [m[:, None, :] for m in jnp.split(mod, 6, axis=-1)]
        h = _rmsnorm(x, p['norm_mix'][i]) * (1 + sc1) + sh1
        j = i // N_MIXERS
        if i % N_MIXERS == 0:
            out, s, cv = _gdn_mixer(h, st_gdn[j], cv_gdn[j], p['gdn_w_in'][j], p['gdn_conv_w'][j], p['gdn_a_log'][j],
                                    p['gdn_dt_bias'][j], p['gdn_norm'][j], p['gdn_w_out'][j])
            new_gdn_s.append(s)
            new_gdn_c.append(cv)
        else:
            out, s, cv = _ssm_mixer(h, st_ssm[j], cv_ssm[j], p['ssm_w_in'][j], p['ssm_conv_w'][j], p['ssm_conv_b'][j],
                                    p['ssm_a_log'][j], p['ssm_dt_bias'][j], p['ssm_d'][j], p['ssm_norm'][j],
                                    p['ssm_w_out'][j])
            new_ssm_s.append(s)
            new_ssm_c.append(cv)
        x = x + gt1 * out
        h = _rmsnorm(x, p['norm_ffn'][i]) * (1 + sc2) + sh2
        x = x + gt2 * _swiglu(h, p['ffn_w_gate_up'][i], p['ffn_w_down'][i])
    y = _rmsnorm(x, p['norm_final'])
    return y, jnp.stack(new_gdn_s), jnp.stack(new_gdn_c), jnp.stack(new_ssm_s), jnp.stack(new_ssm_c)


def setup_inputs(seed: int = 0) -> dict:
    key = jax.random.key(seed)
    ks = jax.random.split(key, 32)
    f32 = jnp.float32
    nrm = lambda k, shape, s: jax.random.normal(k, shape, f32) * s

    def dt_bias_init(k, shape):
        dt = jnp.exp(jax.random.uniform(k, shape, f32, np.log(1e-3), np.log(1e-1)))
        return dt + jnp.log(-jnp.expm1(-dt))

    def a_log_init(k, shape):
        return jnp.log(jax.random.uniform(k, shape, f32, 1.0, 16.0))

    return {
        'x_prompt': nrm(ks[0], (BATCH, SEQ, D_MODEL), 1.0),
        'x_sample': nrm(ks[1], (DEC_BATCH, DEC_SEQ, D_MODEL), 1.0),
        'c_prompt': nrm(ks[2], (BATCH, D_MODEL), 1.0),
        'c_sample': nrm(ks[3], (DEC_BATCH, D_MODEL), 1.0),
        'state_gdn': nrm(ks[4], (N_GDN_LAYERS, DEC_BATCH, GDN_V_HEADS, GDN_HEAD_DIM, GDN_HEAD_DIM), 0.3),
        'state_gdn_conv': nrm(ks[5], (N_GDN_LAYERS, DEC_BATCH, CONV_WIDTH - 1, GDN_CONV_DIM), 1.0),
        'state_ssm': nrm(ks[6], (N_SSM_LAYERS, DEC_BATCH, SSM_HEADS, SSM_HEAD_DIM, SSM_STATE), 0.3),
        'state_ssm_conv': nrm(ks[7], (N_SSM_LAYERS, DEC_BATCH, CONV_WIDTH - 1, SSM_CONV_DIM), 1.0),
        'w_mod': nrm(ks[8], (DEPTH, D_MODEL, 6 * D_MODEL), 0.5 * D_MODEL ** -0.5),
        'b_mod': nrm(ks[9], (DEPTH, 6 * D_MODEL), 0.02),
        'norm_mix': 1.0 + nrm(ks[10], (DEPTH, D_MODEL), 0.02),
        'norm_ffn': 1.0 + nrm(ks[11], (DEPTH, D_MODEL), 0.02),
        'norm_final': 1.0 + nrm(ks[12], (D_MODEL,), 0.02),
        'gdn_w_in': nrm(ks[13], (N_GDN_LAYERS, D_MODEL, GDN_PROJ), D_MODEL ** -0.5),
        'gdn_conv_w': nrm(ks[14], (N_GDN_LAYERS, CONV_WIDTH, GDN_CONV_DIM), CONV_WIDTH ** -0.5),
        'gdn_a_log': a_log_init(ks[15], (N_GDN_LAYERS, GDN_V_HEADS)),
        'gdn_dt_bias': dt_bias_init(ks[16], (N_GDN_LAYERS, GDN_V_HEADS)),
        'gdn_norm': 1.0 + nrm(ks[17], (N_GDN_LAYERS, GDN_HEAD_DIM), 0.02),
        'gdn_w_out': nrm(ks[18], (N_GDN_LAYERS, GDN_V_DIM, D_MODEL), GDN_V_DIM ** -0.5),
        'ssm_w_in': nrm(ks[19], (N_SSM_LAYERS, D_MODEL, SSM_PROJ), D_MODEL ** -0.5),
        'ssm_conv_w': nrm(ks[20], (N_SSM_LAYERS, CONV_WIDTH, SSM_CONV_DIM), CONV_WIDTH ** -0.5),
        'ssm_conv_b': nrm(ks[21], (N_SSM_LAYERS, SSM_CONV_DIM), 0.02),
        'ssm_a_log': a_log_init(ks[22], (N_SSM_LAYERS, SSM_HEADS)),
        'ssm_dt_bias': dt_bias_init(ks[23], (N_SSM_LAYERS, SSM_HEADS)),
        'ssm_d': 1.0 + nrm(ks[24], (N_SSM_LAYERS, SSM_HEADS), 0.1),
        'ssm_norm': 1.0 + nrm(ks[25], (N_SSM_LAYERS, SSM_D_INNER), 0.02),
        'ssm_w_out': nrm(ks[26], (N_SSM_LAYERS, SSM_D_INNER, D_MODEL), SSM_D_INNER ** -0.5),
        'ffn_w_gate_up': nrm(ks[27], (DEPTH, D_MODEL, 2 * FFN_HIDDEN), D_MODEL ** -0.5),
        'ffn_w_down': nrm(ks[28], (DEPTH, FFN_HIDDEN, D_MODEL), FFN_HIDDEN ** -0.5),
    }


def reference(x_prompt, x_sample, c_prompt, c_sample, state_gdn, state_gdn_conv, state_ssm, state_ssm_conv,
              w_mod, b_mod, norm_mix, norm_ffn, norm_final, gdn_w_in, gdn_conv_w, gdn_a_log, gdn_dt_bias, gdn_norm,
              gdn_w_out, ssm_w_in, ssm_conv_w, ssm_conv_b, ssm_a_log, ssm_dt_bias, ssm_d, ssm_norm, ssm_w_out,
              ffn_w_gate_up, ffn_w_down):
    p = dict(w_mod=w_mod, b_mod=b_mod, norm_mix=norm_mix, norm_ffn=norm_ffn, norm_final=norm_final,
             gdn_w_in=gdn_w_in, gdn_conv_w=gdn_conv_w, gdn_a_log=gdn_a_log, gdn_dt_bias=gdn_dt_bias,
             gdn_norm=gdn_norm, gdn_w_out=gdn_w_out, ssm_w_in=ssm_w_in, ssm_conv_w=ssm_conv_w,
             ssm_conv_b=ssm_conv_b, ssm_a_log=ssm_a_log, ssm_dt_bias=ssm_dt_bias, ssm_d=ssm_d, ssm_norm=ssm_norm,
             ssm_w_out=ssm_w_out, ffn_w_gate_up=ffn_w_gate_up, ffn_w_down=ffn_w_down)
    bp = x_prompt.shape[0]
    z_gdn = jnp.zeros((N_GDN_LAYERS, bp) + state_gdn.shape[2:], state_gdn.dtype)
    z_gdn_c = jnp.zeros((N_GDN_LAYERS, bp) + state_gdn_conv.shape[2:], state_gdn_conv.dtype)
    z_ssm = jnp.zeros((N_SSM_LAYERS, bp) + state_ssm.shape[2:], state_ssm.dtype)
    z_ssm_c = jnp.zeros((N_SSM_LAYERS, bp) + state_ssm_conv.shape[2:], state_ssm_conv.dtype)
    y_prompt, gdn_s_p, gdn_c_p, ssm_s_p, ssm_c_p = _trunk(x_prompt, c_prompt, z_gdn, z_gdn_c, z_ssm, z_ssm_c, p)
    y_sample, gdn_s_s, gdn_c_s, ssm_s_s, ssm_c_s = _trunk(x_sample, c_sample, state_gdn, state_gdn_conv,
                                                          state_ssm, state_ssm_conv, p)
    return (y_prompt, y_sample, gdn_s_p, gdn_c_p, ssm_s_p, ssm_c_p, gdn_s_s, gdn_c_s, ssm_s_s, ssm_c_s)
```

```python
import numpy as np
from contextlib import ExitStack
import concourse.bass as bass
import concourse.mybir as mybir
from concourse.bass_utils import run_bass_kernel_spmd

F32 = mybir.dt.float32
BF16 = mybir.dt.bfloat16
AF = mybir.ActivationFunctionType
ALU = mybir.AluOpType

NCORES = 8
D = 1024
NTOK = 2112
NSEQ = 17
EPS = 1e-6
NEG = -30000.0
FFH = 2816
ENGS = ('pe', 'act', 'dve', 'pool', 'sp')


class Tile:
    def __init__(self, name):
        self.name = name
        self.w = None
        self.r = {}
        self.dsem = None
        self.dcnt = 0


class V:
    def __init__(self, t, ap):
        self.t = t
        self.ap = ap

    def __getitem__(self, k):
        return V(self.t, self.ap[k])

    def un(self, ax):
        return V(self.t, self.ap.unsqueeze(ax))

    def bc(self, shape):
        return V(self.t, self.ap.to_broadcast(list(shape)))

    def re(self, s, **kw):
        return V(self.t, self.ap.rearrange(s, **kw))

    def bitcast(self, dt):
        return V(self.t, self.ap.bitcast(dt))


class K:
    def __init__(self):
        self.nc = bass.Bass("TRN2", target_bir_lowering=False)
        self.es = ExitStack()
        self.prog = {e: [] for e in ENGS}
        self.cnt = {e: 0 for e in ENGS}
        self.waited = {e: {} for e in ENGS}
        self.esem = {e: self.es.enter_context(self.nc.semaphore("e_" + e)) for e in ENGS}
        self.final = []
        self.dsems = []
        self.dtiles = []
        self.nbank = 0
        self.banks = []
        self.rotl = list(range(8))
        self.held = set()
        self.bidx = {}
        self.uid = 0

    def sb(self, name, shape, dt):
        t = self.es.enter_context(self.nc.sbuf_tensor("s_" + name, list(shape), dt))
        return V(Tile(name), t[:])

    def alias(self, v, name):
        return V(Tile(name), v.ap)

    def dram(self, name, shape, kind, dt=F32):
        return V(None, self.nc.dram_tensor(name, list(shape), dt, kind=kind).ap())

    def mkbanks(self):
        for i in range(8):
            t = self.es.enter_context(self.nc.psum_tensor("bank%d" % i, [128, 512], F32))
            self.banks.append(V(Tile("bank%d" % i), t[:]))
            self.bidx[id(self.banks[-1].t)] = i

    def bank(self, hold=False):
        for _ in range(len(self.rotl)):
            idx = self.rotl[self.nbank % len(self.rotl)]
            self.nbank += 1
            if idx not in self.held:
                break
        else:
            raise AssertionError("no free PSUM bank")
        if hold:
            self.held.add(idx)
        return self.banks[idx]

    def done(self, b):
        self.held.discard(self.bidx[id(b.t)])

    def reserve(self, n):
        got = self.rotl[-n:]
        self.rotl = self.rotl[:-n]
        return [self.banks[i] for i in got], got

    def release(self, idxs):
        self.rotl = self.rotl + list(idxs)

    def _deps(self, e, reads, writes):
        need = {}

        def add(s, v):
            if need.get(s, 0) < v:
                need[s] = v
        for t in reads:
            if t is not None and t.w is not None:
                add(*t.w)
        for t in writes:
            if t is None:
                continue
            if t.w is not None:
                add(*t.w)
            for s, v in t.r.items():
                add(s, v)
        out = []
        for s, v in need.items():
            if e == 'pe' and s is self.esem['pe']:
                continue
            if self.waited[e].get(s, 0) >= v:
                continue
            self.waited[e][s] = v
            out.append((s, v))
        return out

    def op(self, e, fn, reads, writes, inc=True):
        reads = [v.t for v in reads if v is not None and v.t is not None]
        writes = [v.t for v in writes if v is not None and v.t is not None]
        waits = self._deps(e, reads, writes)
        sem = self.esem[e]
        if inc:
            self.cnt[e] += 1
            ev = (sem, self.cnt[e])
        else:
            ev = (sem, self.cnt[e] + 1)
        for t in reads:
            if t.r.get(ev[0], 0) < ev[1]:
                t.r[ev[0]] = ev[1]
        for t in writes:
            t.w = ev
            t.r = {}

        def emit(h):
            for s, v in waits:
                h.wait_ge(s, v)
            ins = fn(h)
            if inc:
                ins.then_inc(sem, 1)
        self.prog[e].append(emit)

    def dma(self, q, out, in_):
        reads = [in_.t] if in_.t is not None else []
        writes = [out.t] if out.t is not None else []
        waits = self._deps(q, reads, writes)
        st = out.t if out.t is not None else in_.t
        if st.dsem is None:
            st.dsem = self.es.enter_context(self.nc.semaphore("d%d" % len(self.dsems)))
            self.dsems.append(st.dsem)
            self.dtiles.append(st)
        st.dcnt += 16
        ev = (st.dsem, st.dcnt)
        for t in reads:
            if t.r.get(ev[0], 0) < ev[1]:
                t.r[ev[0]] = ev[1]
        for t in writes:
            t.w = ev
            t.r = {}
        if out.t is None:
            self.final.append(ev)
        oa, ia, ds = out.ap, in_.ap, st.dsem

        def emit(h):
            for s, v in waits:
                h.wait_ge(s, v)
            h.dma_start(out=oa, in_=ia).then_inc(ds, 16)
        self.prog[q].append(emit)

    def barrier(self):
        evs = [(self.esem[e], self.cnt[e]) for e in ENGS if self.cnt[e] > 0]
        evs += [(t.dsem, t.dcnt) for t in self.dtiles]
        for e in ENGS:
            waits = []
            for s, v in evs:
                if s is self.esem[e]:
                    continue
                if self.waited[e].get(s, 0) >= v:
                    continue
                self.waited[e][s] = v
                waits.append((s, v))

            def emit(h, waits=waits):
                for s, v in waits:
                    h.wait_ge(s, v)
            self.prog[e].append(emit)

    def mm(self, out, lhsT, rhs, start=True, stop=True, inc=None):
        inc = True
        self.op('pe', lambda h: h.matmul(out.ap, lhsT.ap, rhs.ap, start=start, stop=stop),
                [lhsT, rhs], [out], inc=inc)

    def tr(self, out, in_, ident):
        self.op('pe', lambda h: h.transpose(out.ap, in_.ap, ident.ap), [in_, ident], [out])

    def act(self, out, in_, func, bias=None, scale=1.0, e='act'):
        rd = [in_]
        kw = {}
        if isinstance(bias, V):
            rd.append(bias)
            kw['bias'] = bias.ap
        elif bias is not None:
            kw['bias'] = bias
        if isinstance(scale, V):
            rd.append(scale)
            kw['scale'] = scale.ap
        else:
            kw['scale'] = scale
        self.op('act', lambda h: h.activation(out=out.ap, in_=in_.ap, func=func, **kw), rd, [out])

    def tt(self, e, out, in0, in1, op):
        self.op(e, lambda h: h.tensor_tensor(out=out.ap, in0=in0.ap, in1=in1.ap, op=op), [in0, in1], [out])

    def ts(self, e, out, in0, s1, s2=None, op0=ALU.mult, op1=None):
        rd = [in0]
        a1 = s1.ap if isinstance(s1, V) else s1
        a2 = s2.ap if isinstance(s2, V) else s2
        if isinstance(s1, V):
            rd.append(s1)
        if isinstance(s2, V):
            rd.append(s2)
        if op1 is None:
            self.op(e, lambda h: h.tensor_scalar(out=out.ap, in0=in0.ap, scalar1=a1, scalar2=None, op0=op0), rd, [out])
        else:
            self.op(e, lambda h: h.tensor_scalar(out=out.ap, in0=in0.ap, scalar1=a1, scalar2=a2, op0=op0, op1=op1), rd, [out])

    def stt(self, e, out, in0, scalar, in1, op0, op1):
        rd = [in0, in1]
        a = scalar.ap if isinstance(scalar, V) else scalar
        if isinstance(scalar, V):
            rd.append(scalar)
        self.op(e, lambda h: h.scalar_tensor_tensor(out=out.ap, in0=in0.ap, scalar=a, in1=in1.ap, op0=op0, op1=op1), rd, [out])

    def cp(self, e, out, in_):
        if e == 'act':
            self.act(out, in_, AF.Identity)
        else:
            self.op(e, lambda h: h.tensor_copy(out=out.ap, in_=in_.ap), [in_], [out])

    def memset(self, e, out, val):
        self.op(e, lambda h: h.memset(out.ap, val), [], [out])

    def finish(self):
        nc = self.nc
        fin = {}
        for s, v in self.final:
            if fin.get(id(s), (None, 0))[1] < v:
                fin[id(s)] = (s, v)
        finl = list(fin.values())

        def emit(h):
            for s, v in finl:
                h.wait_ge(s, v)
        self.prog['sp'].append(emit)
        with nc.Block() as block:
            @block.tensor
            def _(h):
                for f in self.prog['pe']:
                    f(h)

            @block.scalar
            def _(h):
                for f in self.prog['act']:
                    f(h)

            @block.vector
            def _(h):
                for f in self.prog['dve']:
                    f(h)

            @block.gpsimd
            def _(h):
                for f in self.prog['pool']:
                    f(h)

            @block.sync
            def _(h):
                for f in self.prog['sp']:
                    f(h)
        self.sbuf_left = nc.sbuf_bytes_remaining
        self.es.close()
        return nc


class Var:
    def __init__(self, kind):
        self.kind = kind
        if kind == 'p':
            self.nt, self.nseq, self.tps, self.niter = 128, 1, 128, 6
        else:
            self.nt, self.nseq, self.tps, self.niter = 64, 16, 4, 1


def host_consts():
    c = {}
    c['ident'] = np.eye(128, dtype=np.float32)
    for kind, nt, tps in (('p', 128, 128), ('s', 64, 4)):
        p = np.arange(128)[:, None]
        f = np.arange(128)[None, :]
        same = (p // tps) == (f // tps)
        valid = (p < nt) & (f < nt)
        ge = np.where(same & (f >= p) & valid, 0.0, NEG).astype(np.float32)
        gt = np.where(same & (f > p) & valid, 0.0, NEG).astype(np.float32)
        um = (same & (p <= f) & valid).astype(np.float32)
        c['mge_' + kind] = np.tile(ge, (1, 4))
        c['mgt_' + kind] = np.tile(gt, (1, 4))
        c['um_' + kind] = um
        c['num_' + kind] = -um
        c['same_' + kind] = (same & valid).astype(np.float32)
    p = np.arange(128)[:, None]
    f = np.arange(128)[None, :]
    lv = np.zeros((128, 7, 128), np.float32)
    for l in range(7):
        b = 1 << l
        lv[:, l, :] = ((p > f) & (p // (2 * b) == f // (2 * b)) & (p // b != f // b)).astype(np.float32)
    c['lvl'] = lv
    c['lvlT'] = np.ascontiguousarray(lv[:, 0, :].T)
    tok = np.arange(64)
    selcol = (tok[:, None] // 4 == np.arange(16)[None, :]).astype(np.float32)
    sc = np.zeros((128, 16), np.float32)
    sc[:64] = selcol
    c['selcol'] = sc
    c['seqrow'] = np.broadcast_to(selcol.T[None, :, :], (128, 16, 64)).astype(np.float32).copy()
    return c


def blk_w(w, bc):
    Kd, N = w.shape
    kc = Kd // 128
    return np.ascontiguousarray(w.reshape(kc, 128, N // bc, bc).transpose(2, 1, 0, 3))


def colvec(v):
    return np.ascontiguousarray(v.reshape(-1, 128).T)


_NC_CACHE = {}


DBG = {'stop': None}


class _Stop(Exception):
    pass


def build():
    k = K()
    nc = k.nc
    ein = lambda n, s: k.dram(n, s, "ExternalInput")
    eout = lambda n, s: k.dram(n, s, "ExternalOutput")
    d_xT = ein("xT", [128, 8, NTOK])
    d_cT = ein("cT", [128, 8, NSEQ])
    d_wmod = ein("wmod", [2, 24, 128, 8, 256])
    d_bmod = ein("bmod", [128, 2, 48])
    d_nmix = ein("nmix", [128, 2, 8])
    d_nffn = ein("nffn", [128, 2, 8])
    d_nfin = ein("nfin", [128, 8])
    d_gwin = ein("gwin", [24, 128, 8, 256])
    d_gwba = ein("gwba", [128, 8, 32])
    d_gcw = ein("gcw", [128, 32, 4])
    d_galog = ein("galog", [128, 16])
    d_gdtb = ein("gdtb", [128, 16])
    d_gnorm = ein("gnorm", [128, 1])
    d_gwout = ein("gwout", [8, 128, 16, 128])
    d_swin = ein("swin", [20, 128, 8, 256])
    d_swdt = ein("swdt", [128, 8, 32])
    d_scw = ein("scw", [128, 24, 4])
    d_scb = ein("scb", [128, 24])
    d_salog = ein("salog", [128, 32])
    d_sdtb = ein("sdtb", [128, 32])
    d_sd = ein("sd", [128, 16])
    d_snorm = ein("snorm", [128, 16])
    d_swout = ein("swout", [8, 128, 16, 128])
    d_fgu = ein("fgu", [2, 22, 128, 8, 256])
    d_fdn = ein("fdn", [2, 8, 128, 22, 128])
    d_gS0 = ein("gS0", [16, 128, 16, 128])
    d_ghalo = ein("ghalo", [128, 32, NSEQ, 3])
    d_sS0 = ein("sS0", [128, 16, 2048])
    d_shalo = ein("shalo", [128, 24, NSEQ, 3])
    hc = host_consts()
    d_c = {n: ein("c_" + n, list(a.shape)) for n, a in hc.items()}
    o_yT = eout("yT", [128, 8, NTOK])
    o_gSp = eout("gSp", [128, 16, 128])
    o_gSs = eout("gSs", [16, 128, 16, 128])
    o_ghalo = eout("ghalo_o", [128, 32, NSEQ, 3])
    o_sSp = eout("sSp", [128, 2048])
    o_sSs = eout("sSs", [128, 16, 2048])
    o_shalo = eout("shalo_o", [128, 24, NSEQ, 3])
    o_dbg = eout("dbg", [128, 16384]) if DBG['stop'] else None

    def dump(v, n, off=0):
        k.dma('sp', o_dbg[:, off:off + n], v)

    def stopat(name, items):
        if DBG['stop'] == name:
            off = 0
            for v, n in items:
                dump(v, n, off)
                off += n
            raise _Stop()

    def dumpb(v, nch, T, off=0):
        k.dma('sp', o_dbg[:, off:off + nch * T // 2].re("p (a b) -> p a b", b=T // 2), v[:, 0:nch, :T].bitcast(F32))

    def dumpf(v, nch, T, off=0):
        k.dma('sp', o_dbg[:, off:off + nch * T].re("p (a b) -> p a b", b=T), v[:, 0:nch, :T])

    k.mkbanks()
    ident = k.sb("ident", [128, 128], F32)
    identb = k.sb("identb", [128, 128], BF16)
    onesf = k.sb("onesf", [128, 128], F32)
    onesb = k.sb("onesb", [128, 128], BF16)
    zerosb = k.sb("zerosb", [128, 512], BF16)
    epsc = k.sb("epsc", [128, 1], F32)
    k.dma('sp', ident, d_c['ident'])
    k.cp('dve', identb, ident)
    k.memset('dve', onesf, 1.0)
    k.memset('dve', onesb, 1.0)
    k.memset('dve', zerosb, 0.0)
    k.memset('dve', epsc, EPS)
    CM = {}
    for kind in ('p', 's'):
        for nm in ('mge', 'mgt'):
            t = k.sb(nm + kind, [128, 512], BF16)
            k.dma('pool', t, d_c[nm + '_' + kind])
            CM[nm + kind] = t
        for nm in ('um', 'num', 'same'):
            t = k.sb(nm + kind, [128, 128], F32)
            k.dma('sp', t, d_c[nm + '_' + kind])
            CM[nm + kind] = t
    lvl = k.sb("lvl", [128, 7, 128], BF16)
    k.dma('pool', lvl, d_c['lvl'])
    lvlT = k.sb("lvlT", [128, 128], BF16)
    k.dma('pool', lvlT, d_c['lvlT'])
    selcol = k.sb("selcol", [128, 16], F32)
    k.dma('sp', selcol, d_c['selcol'])
    seqrow = k.sb("seqrow", [128, 16, 64], BF16)
    k.dma('pool', seqrow, d_c['seqrow'])

    def ld(name, dsrc, shape, dt=F32, q='sp'):
        t = k.sb(name, shape, dt)
        k.dma(q, t, dsrc)
        return t
    bmod = ld("bmod", d_bmod, [128, 2, 48])
    nmix = ld("nmix", d_nmix, [128, 2, 8])
    nffn = ld("nffn", d_nffn, [128, 2, 8])
    nfin = ld("nfin", d_nfin, [128, 8])
    gwba = ld("gwba", d_gwba, [128, 8, 32], BF16, 'pool')
    gcw = ld("gcw", d_gcw, [128, 32, 4])
    galog = ld("galog", d_galog, [128, 16])
    gdtb = ld("gdtb", d_gdtb, [128, 16])
    gnorm = ld("gnorm", d_gnorm, [128, 1])
    swdt = ld("swdt", d_swdt, [128, 8, 32], BF16, 'pool')
    scw = ld("scw", d_scw, [128, 24, 4])
    scb = ld("scb", d_scb, [128, 24])
    salog = ld("salog", d_salog, [128, 32])
    sdtb = ld("sdtb", d_sdtb, [128, 32])
    sdc = ld("sdc", d_sd, [128, 16])
    snorm = ld("snorm", d_snorm, [128, 16])
    ghalo = ld("ghalo", d_ghalo, [128, 32, NSEQ, 3])
    shalo = ld("shalo", d_shalo, [128, 24, NSEQ, 3])
    gnegA = k.sb("gnegA", [128, 16], F32)
    snegA = k.sb("snegA", [128, 32], F32)
    k.act(gnegA, galog, AF.Exp)
    k.ts('dve', gnegA, gnegA, -1.0)
    k.act(snegA, salog, AF.Exp)
    k.ts('dve', snegA, snegA, -1.0)

    NSLOT = 4
    ring = [k.sb("wslot%d" % i, [128, 2048], BF16) for i in range(NSLOT)]
    rstate = {'i': 0}

    NBLK = 136
    wscr = k.nc.dram_tensor("wscr", [NBLK, 128, 2048], BF16).ap()
    rstate['blk'] = None
    rstate['first'] = True

    def wload(dsrc, kc, bc):
        s = ring[rstate['i'] % NSLOT]
        rstate['i'] += 1
        n = kc * bc
        v = V(s.t, s.ap[:, 0:n].rearrange("p (k n) -> p k n", n=bc))
        b = rstate['blk']
        if b is None or DBG['stop']:
            k.dma('pool', v, dsrc)
            return v
        rstate['blk'] = b + 1
        flat = V(s.t, s.ap[:, 0:n])
        if rstate['first']:
            k.dma('pool', v, dsrc)
            k.dma('sp', V(None, wscr[b][:, 0:n]), flat)
        else:
            k.dma('sp', flat, V(None, wscr[b][:, 0:n]))
        return v

    X = k.sb("X", [128, 8, 512], F32)
    H = k.sb("H", [128, 8, 512], BF16)
    QKV = k.sb("QKV", [128, 32, 512], BF16)
    ZS = k.sb("ZS", [128, 16, 512], BF16)
    MOD = [k.sb("MOD%d" % l, [128, 48, NSEQ], F32) for l in range(2)]
    csT = k.sb("csT", [128, 8, NSEQ], BF16)
    cTs = k.sb("cTs", [128, 8, NSEQ], F32)
    gSF = k.sb("gSF", [128, 16, 128], F32)
    sST = k.sb("sST", [128, 2048], F32)
    ARENA = k.sb("arena", [128, 33792], BF16)
    cnt = {'sq': 0, 'ta': 0, 'uc': 0, 'acc': 0, 'rs': 0}
    TMP_OFF = 48128

    def rot(lst, key):
        v = lst[cnt[key] % len(lst)]
        cnt[key] += 1
        return v

    class Arena:
        def __init__(self, tag):
            self.off = 0
            self.tag = tag

        def get(self, name, shape, dt):
            n = int(np.prod(shape[1:]))
            nb = n * (4 if dt == F32 else 2)
            nb = (nb + 63) // 64 * 64
            assert self.off + nb <= 33792 * 2, (self.tag, name, self.off, nb)
            ap = ARENA.ap[:, self.off // 2:(self.off + nb) // 2]
            if dt == F32:
                ap = ap.bitcast(F32)
            ap = ap[:, 0:n]
            if len(shape) == 3:
                ap = ap.rearrange("p (a b) -> p a b", b=shape[2])
            elif len(shape) == 4:
                ap = ap.rearrange("p (a b c) -> p a b c", b=shape[2], c=shape[3])
            self.off += nb
            return V(Tile(self.tag + name), ap)

    _ta = Arena("tmp_")
    _ta.off = TMP_OFF
    sqt = [_ta.get("sqt%d" % i, [128, 512], BF16) for i in range(2)]
    tmpA = [_ta.get("tmpA%d" % i, [128, 512], F32) for i in range(2)]
    rstds = [_ta.get("rstd%d" % i, [128, 512], F32) for i in range(2)]
    Uc = [_ta.get("Uc%d" % i, [128, 515], F32) for i in range(2)]
    acc = [_ta.get("acc%d" % i, [128, 512], F32) for i in range(2)]
    k.dma('sp', cTs, d_cT)
    k.act(csT, cTs, AF.Silu)
    k.memset('dve', gSF, 0.0)
    k.memset('dve', sST, 0.0)
    def mod_block(l, b):
        sv = rstate['blk']
        rstate['blk'] = None
        w = wload(d_wmod[l, b], 8, 256)
        rstate['blk'] = sv
        pb = k.bank()
        for c4 in range(2):
            for kc in range(8):
                k.mm(pb[:, c4 * NSEQ:(c4 + 1) * NSEQ], w[:, kc, c4 * 128:(c4 + 1) * 128], csT[:, kc, :],
                     start=(kc == 0), stop=(kc == 7))
        k.tt('dve', MOD[l][:, 2 * b:2 * b + 2, :], pb[:, 0:2 * NSEQ].re("p (a b) -> p a b", b=NSEQ),
             bmod[:, l, 2 * b:2 * b + 2].un(2).bc([128, 2, NSEQ]), ALU.add)

    def mod_gains(l):
        k.stt('dve', MOD[l][:, 8:16, :], MOD[l][:, 8:16, :], 1.0, nmix[:, l, :].un(2).bc([128, 8, NSEQ]), ALU.add, ALU.mult)
        k.stt('dve', MOD[l][:, 32:40, :], MOD[l][:, 32:40, :], 1.0, nffn[:, l, :].un(2).bc([128, 8, NSEQ]), ALU.add, ALU.mult)

    modq = {'b': 0}
    for b in range(24):
        mod_block(0, b)
    mod_gains(0)
    if DBG['stop']:
        for b in range(24):
            mod_block(1, b)
        mod_gains(1)
        modq['b'] = 25
    if DBG['stop'] == 'setup':
        dump(MOD[0].re("p a b -> p (a b)"), 48 * NSEQ)
        dump(MOD[1].re("p a b -> p (a b)"), 48 * NSEQ, 1024)
        return k.finish(), hc
    def seqview(v, T, var):
        if var.kind == 'p':
            return v.un(1)
        return v.re("p (s t) -> p s t", t=4)

    def rms_rstd(src_chunks, nch, T, scale):
        pb = k.bank()
        for c in range(nch):
            sq = rot(sqt, 'sq')
            k.act(sq[:, :T], src_chunks(c), AF.Square)
            k.mm(pb[:, :T], onesb, sq[:, :T], start=(c == 0), stop=(c == nch - 1))
        ta = rot(tmpA, 'ta')
        rstd = rot(rstds, 'rs')
        k.act(ta[:, :T], pb[:, :T], AF.Ln, scale=scale, bias=epsc[:, 0:1])
        k.act(rstd[:, :T], ta[:, :T], AF.Exp, scale=-0.5)
        return rstd

    def normmod(T, tl, gain, shift, out):
        rstd = rms_rstd(lambda c: X[:, c, :T], 8, T, 1.0 / D)
        for c in range(8):
            if tl['kind'] == 'p':
                if shift is None:
                    k.stt('dve', out[:, c, :T], X[:, c, :T], gain[:, c, 0:1], rstd[:, :T], ALU.mult, ALU.mult)
                else:
                    ta = rot(tmpA, 'ta')
                    k.stt('dve', ta[:, :T], X[:, c, :T], gain[:, c, 0:1], rstd[:, :T], ALU.mult, ALU.mult)
                    k.act(out[:, c, :T], ta[:, :T], AF.Identity, bias=shift[:, c, 0:1])
            else:
                ta = rot(tmpA, 'ta')
                k.tt('dve', ta[:, :T], X[:, c, :T], rstd[:, :T], ALU.mult)
                t3 = ta[:, :T].re("p (s t) -> p s t", t=4)
                k.tt('dve', t3, t3, gain[:, c, 1:17].un(2).bc([128, 16, 4]), ALU.mult)
                o3 = out[:, c, :T].re("p (s t) -> p s t", t=4)
                if shift is None:
                    k.cp('dve', o3, t3)
                else:
                    k.tt('dve', o3, t3, shift[:, c, 1:17].un(2).bc([128, 16, 4]), ALU.add)

    def resid_add(T, tl, pb, c, gate):
        if tl['kind'] == 'p':
            k.stt('dve', X[:, c, :T], pb[:, :T], gate[:, c, 0:1], X[:, c, :T], ALU.mult, ALU.add)
        else:
            ta = rot(tmpA, 'ta')
            t3 = ta[:, :T].re("p (s t) -> p s t", t=4)
            k.tt('dve', t3, pb[:, :T].re("p (s t) -> p s t", t=4), gate[:, c, 1:17].un(2).bc([128, 16, 4]), ALU.mult)
            k.tt('dve', X[:, c, :T], X[:, c, :T], ta[:, :T], ALU.add)

    def conv_chunk(T, tl, pb, halo, c, cw, cb, out):
        u = rot(Uc, 'uc')
        a = rot(acc, 'acc')
        if tl['kind'] == 'p':
            nseq, tps, s0 = 1, T, 0
        else:
            nseq, tps, s0 = 16, 4, 1
        w = 3 + tps
        u3 = u[:, 0:nseq * w].re("p (s t) -> p s t", t=w)
        k.cp('dve', u3[:, :, 0:3], halo[:, c, s0:s0 + nseq, :])
        k.act(u3[:, :, 3:w], pb[:, :T].re("p (s t) -> p s t", t=tps), AF.Identity)
        a3 = a[:, :T].re("p (s t) -> p s t", t=tps)
        k.act(a3, u3[:, :, 3:w], AF.Copy, scale=cw[:, c, 3:4])
        for j in (0, 1, 2):
            k.stt('dve', a3, u3[:, :, j:j + tps], cw[:, c, j:j + 1], a3, ALU.mult, ALU.add)
        k.cp('act', halo[:, c, s0:s0 + nseq, :], u3[:, :, tps:tps + 3])

        def fin():
            if cb is None:
                k.act(out[:, c, :T], a[:, :T], AF.Silu)
            else:
                k.act(out[:, c, :T], a[:, :T], AF.Silu, bias=cb[:, c:c + 1])
        return fin

    def softplus_cols(dst, src_ps, bias_bc, nt, n, tmp):
        k.tt('dve', tmp[:nt, :n], src_ps, bias_bc[:nt, :n], ALU.add)
        k.act(tmp[:nt, :n], tmp[:nt, :n], AF.Exp)
        k.act(dst, tmp[:nt, :n], AF.Ln, bias=1.0)

    def ffn(l, T, tl):
        ar = Arena("ffn%d_" % l)
        ACTB = ar.get("act", [128, 22, 512], BF16)
        normmod(T, tl, MOD[l][:, 32:40, :], MOD[l][:, 24:32, :], H)
        for b in range(22):
            w = wload(d_fgu[l, b], 8, 256)
            pg = k.bank()
            pu = k.bank()
            for kc in range(8):
                k.mm(pg[:, :T], w[:, kc, 0:128], H[:, kc, :T], start=(kc == 0), stop=(kc == 7))
            for kc in range(8):
                k.mm(pu[:, :T], w[:, kc, 128:256], H[:, kc, :T], start=(kc == 0), stop=(kc == 7))
            sg = rot(sqt, 'sq')
            k.act(sg[:, :T], pg[:, :T], AF.Silu)
            k.tt('dve', ACTB[:, b, :T], pu[:, :T], sg[:, :T], ALU.mult)
        for dch in range(8):
            pb = k.bank()
            for hf in range(2):
                w = wload(d_fdn[l, dch][:, 11 * hf:11 * hf + 11, :], 11, 128)
                for kk in range(11):
                    kc = 11 * hf + kk
                    k.mm(pb[:, :T], w[:, kk, :], ACTB[:, kc, :T], start=(kc == 0), stop=(kc == 21))
            resid_add(T, tl, pb, dch, MOD[l][:, 40:48, :])

    def out_proj(l, T, tl, dw, src):
        for b in range(8):
            w = wload(dw[b], 16, 128)
            pb = k.bank()
            for kc in range(16):
                k.mm(pb[:, :T], w[:, kc, :], src[:, kc, :T], start=(kc == 0), stop=(kc == 15))
            resid_add(T, tl, pb, b, MOD[l][:, 16:24, :])

    def pipeline(gens, depth):
        active = []
        it = iter(gens)
        done = False
        while True:
            if len(active) < depth and not done:
                g_ = next(it, None)
                if g_ is None:
                    done = True
                else:
                    active.append(g_)
            if not active:
                if done:
                    break
                continue
            for g_ in list(active):
                try:
                    next(g_)
                except StopIteration:
                    active.remove(g_)

    def exp_args(ar, var, gcols, lbcols, want_e):
        nt = var.nt
        kd = var.kind
        RXc = ar['RXc']
        gb = gcols.un(2).bc([nt, 4, nt])
        k.tt('pool', RXc[:nt, :, :nt], gb, CM['um' + kd][:nt, :nt].un(1).bc([nt, 4, nt]), ALU.mult)
        if want_e:
            RXe = ar['RXe']
            k.tt('pool', RXe[:nt, :, :nt], lbcols.un(2).bc([nt, 4, nt]), ident[:nt, :nt].un(1).bc([nt, 4, nt]), ALU.mult)
        yield
        N = 4 * nt
        r2 = lambda v: v[:nt, :, :nt]
        pd = k.bank(hold=True)
        pdv = pd[:nt, 0:N].re("p (h i) -> p h i", i=nt)
        k.mm(pdv, onesf[:nt, :nt], r2(RXc), start=True, stop=False)
        k.mm(pdv, CM['num' + kd][:nt, :nt], gb, start=False, stop=False)
        k.mm(pdv, identb[:nt, :nt], CM['mge' + kd][:nt, 0:512].re("p (h i) -> p h i", i=128)[:, :, :nt], start=False, stop=True)
        pe_ = k.bank(hold=True)
        pev = pe_[:, 0:N].re("p (h i) -> p h i", i=nt)
        k.mm(pev, onesf[:nt, :], r2(RXc), start=True, stop=True)
        if want_e:
            pq = k.bank(hold=True)
            pqv = pq[:nt, 0:N].re("p (h i) -> p h i", i=nt)
            k.mm(pqv, onesf[:nt, :nt], r2(RXc), start=True, stop=False)
            k.mm(pqv, onesf[:nt, :nt], r2(RXe), start=False, stop=False)
            k.mm(pqv, CM['num' + kd][:nt, :nt], gb, start=False, stop=False)
            k.mm(pqv, identb[:nt, :nt], CM['mgt' + kd][:nt, 0:512].re("p (h i) -> p h i", i=128)[:, :, :nt], start=False, stop=True)
        yield
        k.act(ar['DT'][:nt, :, :nt], pdv, AF.Exp)
        k.act(ar['EB'][:, :, :nt], pev, AF.Exp)
        k.done(pd)
        k.done(pe_)
        if want_e:
            k.act(ar['E'][:nt, :, :nt], pqv, AF.Exp)
            k.done(pq)
        yield

    def gdn(T, tl):
        l = 0
        normmod(T, tl, MOD[l][:, 8:16, :], MOD[l][:, 0:8, :], H)
        if DBG['stop'] == 'norm0':
            dumpb(H, 8, T)
            return True
        pendf = [None]
        for b in range(24):
            w = wload(d_gwin[b], 8, 256)
            for j in range(2):
                ch = 2 * b + j
                pb = k.bank()
                for kc in range(8):
                    k.mm(pb[:, :T], w[:, kc, j * 128:(j + 1) * 128], H[:, kc, :T], start=(kc == 0), stop=(kc == 7))
                if ch < 32:
                    fin = conv_chunk(T, tl, pb, ghalo, ch, gcw, None, QKV)
                    if pendf[0] is not None:
                        pendf[0]()
                    pendf[0] = fin
                else:
                    if pendf[0] is not None:
                        pendf[0]()
                        pendf[0] = None
                    k.act(ZS[:, ch - 32, :T], pb[:, :T], AF.Silu)
        pend = None
        for c2 in range(17):
            cur = None
            if c2 < 16:
                pb = k.bank(hold=True)
                sq = rot(sqt, 'sq')
                k.act(sq[:, :T], QKV[:, c2, :T], AF.Square)
                k.mm(pb[:, :T], onesb, sq[:, :T], start=True, stop=True)
                cur = (c2, pb)
            if pend is not None:
                c1, pb1 = pend
                ta = rot(tmpA, 'ta')
                rstd = rot(rstds, 'rs')
                k.act(ta[:, :T], pb1[:, :T], AF.Ln, scale=1.0, bias=epsc[:, 0:1])
                k.done(pb1)
                k.act(rstd[:, :T], ta[:, :T], AF.Exp, scale=-0.5)
                if c1 < 8:
                    k.stt('dve', QKV[:, c1, :T], QKV[:, c1, :T], 128.0 ** -0.5, rstd[:, :T], ALU.mult, ALU.mult)
                else:
                    k.tt('dve', QKV[:, c1, :T], QKV[:, c1, :T], rstd[:, :T], ALU.mult)
            pend = cur
        if DBG['stop'] == 'gdn_in':
            dumpb(QKV, 32, T)
            dumpb(ZS, 4, T, 8192)
            return True
        k.barrier()
        var = Var(tl['kind'])
        nt = var.nt
        kd = var.kind
        A = Arena("gdn_")
        NCTX = 3 if kd == 'p' else 1
        ctxs = []
        shRX = [dict(RXc=A.get('RXc_%d' % i, [128, 4, 128], F32), RXe=A.get('RXe_%d' % i, [128, 4, 128], F32))
                for i in range(1)]
        for ci_ in range(NCTX):
            c_ = dict(shRX[ci_ % len(shRX)])
            for nm in ('EB',):
                c_[nm] = A.get('%s_%d' % (nm, ci_), [128, 4, 128], F32)
            for nm in ('DT', 'E', 'T1', 'Yn', 'Tb0', 'Tb1', 'Wb0', 'Wb1', 'Rv', 'Rk', 'Kd', 'QD', 'QK'):
                c_[nm] = A.get('%s_%d' % (nm, ci_), [128, 4, 128], BF16)
            ctxs.append(c_)
        NHB = 2 if kd == 'p' else 1
        for ci_, c_ in enumerate(ctxs):
            hbl = []
            for i in range(NHB):
                ot = A.get('otmp%d_%d' % (i, ci_), [128, 128], F32)
                hbl.append(dict(nwT=A.get('nwT%d_%d' % (i, ci_), [128, 128], BF16),
                                delta=A.get('delta%d_%d' % (i, ci_), [128, 128], BF16),
                                osq=A.get('osq%d_%d' % (i, ci_), [128, 128], BF16), otmp=ot, orstd=ot,
                                otmp2=A.get('otmp2%d_%d' % (i, ci_), [128, 128], F32)))
            c_['hb'] = hbl
        CSb = [A.get('CS%d' % i, [128, 10, 16], F32) for i in range(2)]
        CS = CSb[0]
        gSB = A.get('gSB', [128, 16, 128], BF16)
        tmpc = A.get('tmpc', [128, 32], F32)
        if kd == 's':
            nwTz = A.get('nwTz', [128, 16, 64], BF16)
            kdz = A.get('kdz', [128, 16, 128], BF16)
            SFsb = [A.get('SFs%d' % i, [128, 16, 128], F32) for i in range(2)]
            SBsb = [A.get('SBs%d' % i, [128, 16, 128], BF16) for i in range(2)]
            gpf = {'n': 0}

            def gload(h):
                if h == gpf['n'] and h < 16:
                    k.dma('sp', SFsb[h % 2], d_gS0[h])
                    gpf['n'] += 1
        nch = DBG.get('nch', T // nt)
        gdone = {}
        cast_done = {}
        ctx_busy = {}

        def chunk_stats(ci):
            cols = slice(ci * nt, (ci + 1) * nt)
            CS = CSb[ci % 2]
            c_nl, c_lb, c_beta, c_g, c_cum, c_tot, c_bec, c_kdec, c_t1 = [CS[:, i, :] for i in range(9)]
            pg = k.bank()
            for kc in range(8):
                k.mm(pg[:nt, 0:32], H[:, kc, cols], gwba[:, kc, :], start=(kc == 0), stop=(kc == 7))
            k.act(tmpc[:nt, 0:16], pg[:nt, 0:16], AF.Exp, scale=-1.0)
            k.act(c_nl[:nt], tmpc[:nt, 0:16], AF.Ln, bias=1.0)
            k.ts('dve', c_lb[:nt], c_nl[:nt], -1.0)
            k.act(c_beta[:nt], c_nl[:nt], AF.Exp, scale=-1.0)
            softplus_cols(c_t1[:nt], pg[:nt, 16:32], gdtb, nt, 16, tmpc[:, 16:32])
            k.tt('dve', c_g[:nt], c_t1[:nt], gnegA[:nt], ALU.mult)
            pc = k.bank()
            k.mm(pc[:nt, 0:16], CM['um' + kd][:nt, :nt], c_g[:nt], start=True, stop=True)
            k.mm(pc[:nt, 16:32], CM['same' + kd][:nt, :nt], c_g[:nt], start=True, stop=True)
            k.cp('dve', c_cum[:nt], pc[:nt, 0:16])
            k.cp('dve', c_tot[:nt], pc[:nt, 16:32])
            k.act(c_t1[:nt], c_cum[:nt], AF.Exp)
            k.tt('dve', c_bec[:nt], c_t1[:nt], c_beta[:nt], ALU.mult)
            k.tt('dve', c_t1[:nt], c_tot[:nt], c_cum[:nt], ALU.subtract)
            k.act(c_kdec[:nt], c_t1[:nt], AF.Exp)
            stopat('g_a', [(CS.re("p a b -> p (a b)"), 160)])
            yield

        if True:
            def group(g, cx, ci):
                cols = slice(ci * nt, (ci + 1) * nt)
                CS = CSb[ci % 2]
                c_nl, c_lb, c_beta, c_g, c_cum, c_tot, c_bec, c_kdec, c_t1 = [CS[:, i, :] for i in range(9)]
                assert not ctx_busy.get(id(cx), False)
                ctx_busy[id(cx)] = True
                if DBG.get('frontwait'):
                    while ci > 0 and gdone.get(ci - 1, 0) < ng_:
                        yield
                hs = slice(4 * g, 4 * g + 4)
                for _ in range(2):
                    if modq['b'] < 24:
                        mod_block(1, modq['b'])
                        modq['b'] += 1
                for _ in exp_args(cx, var, c_g[:nt, hs], c_lb[:nt, hs], True):
                    yield
                DT, E, EB = cx['DT'], cx['E'], cx['EB']
                Rv, Rk, Kd, QD, QK = cx['Rv'], cx['Rk'], cx['Kd'], cx['QD'], cx['QK']
                Tb = [cx['Tb0'], cx['Tb1']]
                Wb = [cx['Wb0'], cx['Wb1']]
                pG = k.bank()
                pQ = k.bank()
                for e in range(2):
                    qh = 2 * g + e
                    kT = QKV[:, 8 + qh, cols]
                    qT = QKV[:, qh, cols]
                    k.mm(pG[:nt, e * nt:(e + 1) * nt], kT, kT, start=True, stop=True)
                    k.mm(pQ[:nt, e * nt:(e + 1) * nt], kT, qT, start=True, stop=True)
                T1 = cx['T1']
                for e in range(2):
                    Gv = pG[:nt, e * nt:(e + 1) * nt].un(1).bc([nt, 2, nt])
                    k.stt('dve', T1[:nt, 2 * e:2 * e + 2, :nt], Gv, -1.0, E[:nt, 2 * e:2 * e + 2, :nt], ALU.mult, ALU.mult)
                    Qv = pQ[:nt, e * nt:(e + 1) * nt].un(1).bc([nt, 2, nt])
                    k.tt('dve', QK[:nt, 2 * e:2 * e + 2, :nt], Qv, DT[:nt, 2 * e:2 * e + 2, :nt], ALU.mult)
                yield
                Yn = cx['Yn']
                Wc, Wtc = Wb[1], Tb[1]
                pT = k.bank(hold=True)
                pTb = pT.bitcast(BF16)
                for hl in range(4):
                    k.tr(pTb[:nt, hl * nt:(hl + 1) * nt], T1[:nt, hl, :nt], identb[:nt, :nt])
                k.tt('pool', Wtc[:nt, :, :nt], T1[:nt, :, :nt], lvlT[:nt, 0:nt].un(1).bc([nt, 4, nt]), ALU.mult)
                k.tt('pool', Wtc[:nt, :, :nt], Wtc[:nt, :, :nt], identb[:nt, :nt].un(1).bc([nt, 4, nt]), ALU.add)
                yield
                k.tt('dve', Wc[:nt, :, :nt], pTb[:nt, 0:4 * nt].re("p (h i) -> p h i", i=nt),
                     lvl[:nt, 0, :nt].un(1).bc([nt, 4, nt]), ALU.mult)
                k.done(pT)
                k.tt('pool', Wc[:nt, :, :nt], Wc[:nt, :, :nt], identb[:nt, :nt].un(1).bc([nt, 4, nt]), ALU.add)
                yield
                nlev = 7 if kd == 'p' else 2
                for lev in range(1, nlev):
                    Wn, Wtn = Wb[lev % 2], Tb[lev % 2]
                    pY = k.bank(hold=True)
                    for hl in range(4):
                        k.mm(pY[:nt, hl * nt:(hl + 1) * nt], T1[:nt, hl, :nt], Wc[:nt, hl, :nt], start=True, stop=True)
                    yield
                    k.tt('dve', Yn[:nt, :, :nt], pY[:nt, 0:4 * nt].re("p (h i) -> p h i", i=nt),
                         lvl[:nt, lev, :nt].un(1).bc([nt, 4, nt]), ALU.mult)
                    k.done(pY)
                    lastl = (lev == nlev - 1)
                    if not lastl:
                        pW = k.bank(hold=True)
                        for hl in range(4):
                            k.mm(pW[:nt, hl * nt:(hl + 1) * nt], Wtc[:nt, hl, :nt], Yn[:nt, hl, :nt], start=True, stop=True)
                    pWt = k.bank(hold=True)
                    for hl in range(4):
                        k.mm(pWt[:nt, hl * nt:(hl + 1) * nt], Yn[:nt, hl, :nt], Wtc[:nt, hl, :nt], start=True, stop=True)
                    yield
                    if not lastl:
                        k.tt('dve', Wn[:nt, :, :nt], pW[:nt, 0:4 * nt].re("p (h i) -> p h i", i=nt), Wc[:nt, :, :nt], ALU.add)
                        k.done(pW)
                    k.tt('dve', Wtn[:nt, :, :nt], pWt[:nt, 0:4 * nt].re("p (h i) -> p h i", i=nt), Wtc[:nt, :, :nt], ALU.add)
                    k.done(pWt)
                    Wc, Wtc = Wn, Wtn
                    yield
                Wc = Wtc
                pv = k.bank(hold=True)
                pvb = pv.bitcast(BF16)
                for hl in range(4):
                    k.tr(pvb[:nt, hl * 128:(hl + 1) * 128], QKV[:, 16 + 4 * g + hl, cols], identb)
                for e in range(2):
                    k.tr(pvb[:nt, 512 + e * 128:512 + (e + 1) * 128], QKV[:, 8 + 2 * g + e, cols], identb)
                for e in range(2):
                    qv = QKV[:, 2 * g + e, cols].un(1).bc([128, 2, nt])
                    k.tt('pool', QD[:, 2 * e:2 * e + 2, :nt], qv, EB[:, 2 * e:2 * e + 2, :nt], ALU.mult)
                yield
                for hl in range(4):
                    h = 4 * g + hl
                    e = hl // 2
                    kvp = pvb[:nt, 512 + e * 128:512 + (e + 1) * 128]
                    k.act(Rv[:nt, hl, :], pvb[:nt, hl * 128:(hl + 1) * 128], AF.Copy, scale=c_beta[:nt, h:h + 1])
                    k.act(Rk[:nt, hl, :], kvp, AF.Copy, scale=c_bec[:nt, h:h + 1])
                    k.act(Kd[:nt, hl, :], kvp, AF.Copy, scale=c_kdec[:nt, h:h + 1])
                k.done(pv)
                yield
                if kd == 'p':
                    while ci > 0 and gdone.get(ci - 1, 0) < ng_:
                        yield
                    if g == 0:
                        k.cp('act', gSB, gSF)
                        cast_done[ci] = True
                    while not cast_done.get(ci, False):
                        yield
                for hl in range(4):
                    h = 4 * g + hl
                    hbs = cx['hb'][hl % NHB]
                    nwT, delta, osq, otmp, orstd, otmp2 = (hbs['nwT'], hbs['delta'], hbs['osq'], hbs['otmp'],
                                                           hbs['orstd'], hbs['otmp2'])
                    if kd == 's':
                        gload(h)
                        gload(h + 1)
                        SFs, SBs = SFsb[h % 2], SBsb[h % 2]
                        k.cp('act', SBs, SFs)
                        Sf = lambda s, SFs=SFs: SFs[:, s, :]
                        Sb = lambda s, SBs=SBs: SBs[:, s, :]
                    else:
                        Sf = lambda s, h=h: gSF[:, h, :]
                        Sb = lambda s, h=h: gSB[:, h, :]
                    pw = k.bank()
                    k.mm(pw[:, :nt], Rk[:nt, hl, :], Wc[:nt, hl, :nt], start=True, stop=True)
                    k.act(nwT[:, :nt], pw[:, :nt], AF.Identity, scale=-1.0)
                    if kd == 's':
                        k.tt('dve', nwTz[:, :, :nt], nwT[:, :nt].un(1).bc([128, 16, nt]), seqrow[:, :, :nt], ALU.mult)
                        k.tt('dve', kdz[:nt], Kd[:nt, hl, :].un(1).bc([nt, 16, 128]),
                             selcol[:nt, :].un(2).bc([nt, 16, 128]), ALU.mult)
                    pdl = k.bank()
                    k.mm(pdl[:nt, 0:128], Wc[:nt, hl, :nt], Rv[:nt, hl, :], start=True, stop=False)
                    for s in range(var.nseq):
                        lw = nwTz[:, s, :nt] if kd == 's' else nwT[:, :nt]
                        k.mm(pdl[:nt, 0:128], lw, Sb(s), start=False, stop=(s == var.nseq - 1))
                    k.cp('act', delta[:nt], pdl[:nt, 0:128])
                    yield
                    po = k.bank()
                    k.mm(po[:, :nt], delta[:nt], QK[:nt, hl, :nt], start=True, stop=False)
                    for s in range(var.nseq):
                        cs = slice(s * var.tps, (s + 1) * var.tps)
                        k.mm(po[:, cs], Sb(s), QD[:, hl, cs], start=False, stop=(s == var.nseq - 1))
                    k.act(osq[:, :nt], po[:, :nt], AF.Square)
                    pn = k.bank()
                    k.mm(pn[:, :nt], onesb, osq[:, :nt], start=True, stop=True)
                    k.act(otmp[:, :nt], pn[:, :nt], AF.Ln, scale=1.0 / 128, bias=epsc[:, 0:1])
                    k.act(orstd[:, :nt], otmp[:, :nt], AF.Exp, scale=-0.5)
                    k.tt('dve', otmp2[:, :nt], po[:, :nt], orstd[:, :nt], ALU.mult)
                    k.stt('dve', ZS[:, h, cols], otmp2[:, :nt], gnorm[:, 0:1], ZS[:, h, cols], ALU.mult, ALU.mult)
                    for s in range(var.nseq):
                        ps = k.bank()
                        lk = kdz[:nt, s, :] if kd == 's' else Kd[:nt, hl, :]
                        k.mm(ps[:, 0:128], lk, delta[:nt], start=True, stop=True)
                        lastc = (s + 1) * var.tps - 1
                        k.stt('dve', Sf(s), Sf(s), EB[:, hl, lastc:lastc + 1], ps[:, 0:128], ALU.mult, ALU.add)
                    if kd == 's':
                        k.dma('sp', o_gSs[h], SFs)
                    yield
                ctx_busy[id(cx)] = False
                gdone[ci] = gdone.get(ci, 0) + 1

            ng_ = DBG.get('ng', 4)
            gens = []
            for ci in range(nch):
                gens.append(chunk_stats(ci))
                for g in range(ng_):
                    gi = ci * ng_ + g
                    gens.append(group(g, ctxs[gi % NCTX], ci))
            pipeline(gens, NCTX)
        while modq['b'] < 24:
            mod_block(1, modq['b'])
            modq['b'] += 1
        if modq['b'] == 24:
            mod_gains(1)
            modq['b'] = 25
        if tl['last_p']:
            k.dma('sp', o_gSp, gSF)
        if DBG['stop'] == 'gdn_mix':
            dumpb(ZS, 16, T)
            dump(gSF.re("p a b -> p (a b)"), 2048, 8192)
            return True
        k.barrier()
        out_proj(0, T, tl, d_gwout, ZS)

    def ssd(T, tl):
        l = 1
        normmod(T, tl, MOD[l][:, 8:16, :], MOD[l][:, 0:8, :], H)
        XBC = QKV
        pendf = [None]
        for b in range(20):
            w = wload(d_swin[b], 8, 256)
            for j in range(2):
                ch = 2 * b + j
                pb = k.bank()
                for kc in range(8):
                    k.mm(pb[:, :T], w[:, kc, j * 128:(j + 1) * 128], H[:, kc, :T], start=(kc == 0), stop=(kc == 7))
                if ch < 16:
                    k.act(ZS[:, ch, :T], pb[:, :T], AF.Silu)
                else:
                    fin = conv_chunk(T, tl, pb, shalo, ch - 16, scw, scb, XBC)
                    if pendf[0] is not None:
                        pendf[0]()
                    pendf[0] = fin
        if pendf[0] is not None:
            pendf[0]()
            pendf[0] = None
        if DBG['stop'] == 'ssd_in':
            dumpb(QKV, 24, T)
            dumpb(ZS, 4, T, 8192)
            return True
        k.barrier()
        var = Var(tl['kind'])
        nt = var.nt
        kd = var.kind
        A = Arena("ssd_")
        ars = []
        for i_ in range(2):
            ars.append(dict(RXc=A.get('RXc%d' % i_, [128, 4, 128], F32), EB=A.get('EB%d' % i_, [128, 4, 128], F32),
                            DT=A.get('DT%d' % i_, [128, 4, 128], BF16)))
        MT = A.get('MT', [128, 32, nt], BF16)
        CE = A.get('CE', [128, 32, nt], BF16)
        XDTz = A.get('XDTz', [128, 16, 2, 128], BF16)
        XDTD = A.get('XDTD', [128, 2048], BF16)
        BTM = A.get('BTM', [128, 4, 128], BF16)
        CBs = A.get('CBs', [128, 4, 128], BF16)
        STz = A.get('STz', [128, 32, 128], BF16)
        CS = A.get('CS', [128, 8, 32], F32)
        EL = A.get('EL', [128, 32, 16], F32)
        ytmp = A.get('ytmp', [128, 128], F32)
        ybuf = A.get('ybuf', [128, 4, 128], F32) if kd == 'p' else None
        ysqs = [A.get('ysq%d' % i, [128, 128], BF16) for i in range(4)]
        yrs = [A.get('yrs%d' % i, [128, 128], F32) for i in range(2)]
        if kd == 's':
            STsb = [A.get('STs%d' % i, [128, 2048], F32) for i in range(2)]
            spf = {'n': 0}

            def sload(s_):
                if s_ == spf['n'] and s_ < 16:
                    k.dma('sp', STsb[s_ % 2], d_sS0[:, s_, :])
                    spf['n'] += 1
            XDz = A.get('XDz', [128, 2048], BF16)
        tmpc = A.get('tmpc', [128, 32], F32)
        c_dt, c_la, c_cum, c_tot, c_dtd, c_t1 = [CS[:, i, :] for i in range(6)]
        k.memset('dve', XDTz, 0.0)
        k.memset('dve', STz, 0.0)
        nch = T // nt
        for ci in range(nch):
            cols = slice(ci * nt, (ci + 1) * nt)
            pg = k.bank()
            for kc in range(8):
                k.mm(pg[:nt, 0:32], H[:, kc, cols], swdt[:, kc, :], start=(kc == 0), stop=(kc == 7))
            softplus_cols(c_dt[:nt], pg[:nt, 0:32], sdtb, nt, 32, tmpc)
            k.tt('dve', c_la[:nt], c_dt[:nt], snegA[:nt], ALU.mult)
            pc = k.bank()
            k.mm(pc[:nt, 0:32], CM['um' + kd][:nt, :nt], c_la[:nt], start=True, stop=True)
            k.mm(pc[:nt, 32:64], CM['same' + kd][:nt, :nt], c_la[:nt], start=True, stop=True)
            k.cp('dve', c_cum[:nt], pc[:nt, 0:32])
            k.cp('dve', c_tot[:nt], pc[:nt, 32:64])
            k.tt('dve', c_t1[:nt], c_tot[:nt], c_cum[:nt], ALU.subtract)
            k.act(c_t1[:nt], c_t1[:nt], AF.Exp)
            k.tt('dve', c_dtd[:nt], c_t1[:nt], c_dt[:nt], ALU.mult)
            pCB = k.bank()
            for g in range(4):
                k.mm(pCB[:nt, g * nt:(g + 1) * nt], XBC[:, 16 + g, cols], XBC[:, 20 + g, cols], start=True, stop=True)
            k.cp('act', CBs[:nt, :, :nt], pCB[:nt, 0:4 * nt].re("p (g i) -> p g i", i=nt))
            def sgroup(g8, ar, cols=cols):
                hs = slice(4 * g8, 4 * g8 + 4)
                for _ in exp_args(ar, var, c_la[:nt, hs], None, False):
                    yield
                grp = g8 // 2
                k.tt('dve', MT[:nt, hs, :nt], CBs[:nt, grp, :nt].un(1).bc([nt, 4, nt]),
                     ar['DT'][:nt, :, :nt], ALU.mult)
                k.tt('dve', CE[:, hs, :nt], XBC[:, 20 + grp, cols].un(1).bc([128, 4, nt]), ar['EB'][:, :, :nt], ALU.mult)
                k.cp('dve', EL[:, hs, 0:var.nseq], ar['EB'][:, :, var.tps - 1:nt:var.tps])
                yield
                if g8 < 4:
                    q4 = g8
                    px = k.bank(hold=True)
                    pxb = px.bitcast(BF16)
                    for j in range(4):
                        k.tr(pxb[:nt, j * 128:(j + 1) * 128], XBC[:, 4 * q4 + j, cols], identb)
                    yield
                    xv = pxb[:nt, 0:512].re("p (h d) -> p h d", d=64)
                    h8 = slice(8 * q4, 8 * q4 + 8)
                    xv4 = pxb[:nt, 0:512].re("p (a e d) -> p a e d", e=2, d=64)
                    for e in range(2):
                        k.tt('dve', XDTz[:nt, 4 * q4:4 * q4 + 4, e, e * 64:(e + 1) * 64], xv4[:, :, e, :],
                             c_dt[:nt, 8 * q4 + e:8 * q4 + 8:2].un(2).bc([nt, 4, 64]), ALU.mult)
                    k.tt('dve', XDTD[:nt, 512 * q4:512 * (q4 + 1)].re("p (h d) -> p h d", d=64), xv,
                         c_dtd[:nt, h8].un(2).bc([nt, 8, 64]), ALU.mult)
                    k.done(px)
                    yield
                elif g8 == 4:
                    pbm = k.bank(hold=True)
                    pbb = pbm.bitcast(BF16)
                    for g in range(4):
                        k.tr(pbb[:nt, g * 128:(g + 1) * 128], XBC[:, 16 + g, cols], identb)
                    yield
                    k.cp('act', BTM[:nt], pbb[:nt, 0:512].re("p (g n) -> p g n", n=128))
                    k.done(pbm)
                    yield

            pipeline([sgroup(g8, ars[g8 % 2]) for g8 in range(8)], 2)
            ybanks, yidx = k.reserve((16 * nt * 4 + 2047) // 2048)
            per = 512 // nt

            def yreg(pr):
                return ybanks[pr // per][:, (pr % per) * nt:(pr % per + 1) * nt]
            for yb in ybanks:
                k.mm(yb[:, 0:512], zerosb[:, 0:128], zerosb[:, 0:512], start=True, stop=False)
            for pr in range(16):
                for e in range(2):
                    k.mm(yreg(pr), XDTz[:nt, pr, e, :], MT[:nt, 2 * pr + e, :nt], start=False, stop=False)
            for s in range(var.nseq):
                if kd == 's':
                    sload(s)
                    sload(s + 1)
                    STs = STsb[s % 2]
                    ST = STs
                else:
                    ST = sST
                STz4 = STz.re("p (a e) c -> p a e c", e=2)
                ST4 = ST.re("p (a e d) -> p a e d", e=2, d=64)
                k.cp('act', STz4[:, :, 0, 0:64], ST4[:, :, 0, :])
                k.cp('dve', STz4[:, :, 1, 64:128], ST4[:, :, 1, :])
                cs = slice(s * var.tps, (s + 1) * var.tps)
                for pr in range(16):
                    for e in range(2):
                        lastmm = (s == var.nseq - 1 and e == 1 and (pr % per == per - 1 or pr == 15))
                        reg = yreg(pr)
                        k.mm(reg[:, cs], STz[:, 2 * pr + e, :], CE[:, 2 * pr + e, cs], start=False, stop=lastmm)
                if kd == 's':
                    k.ts('dve', XDz[:nt], XDTD[:nt], selcol[:nt, s:s + 1])
                    xd = XDz
                else:
                    xd = XDTD
                for g in range(4):
                    pu = k.bank()
                    k.mm(pu[:, 0:512], BTM[:nt, g, :], xd[:nt, 512 * g:512 * (g + 1)], start=True, stop=True)
                    stv = ST[:, 512 * g:512 * (g + 1)].re("p (h d) -> p h d", d=64)
                    k.tt('dve', stv, stv, EL[:, 8 * g:8 * g + 8, s:s + 1].bc([128, 8, 64]), ALU.mult)
                    k.tt('dve', ST[:, 512 * g:512 * (g + 1)], ST[:, 512 * g:512 * (g + 1)], pu[:, 0:512], ALU.add)
                if kd == 's':
                    k.dma('sp', o_sSs[:, s, :], STs)
            if DBG['stop'] == 'ssd_y' and ci == DBG.get('ci', 0):
                for pr in range(16):
                    k.cp('dve', ytmp[:, :nt], yreg(pr))
                    dump(ytmp[:, :nt], nt, pr * nt)
                dump(CS[:, 0:6, :].re("p a b -> p (a b)"), 192, 16 * nt)
                raise _Stop()
            if kd == 'p':
                for q in range(4):
                    p4 = slice(4 * q, 4 * q + 4)
                    k.tt('pool', ybuf[:, :, :nt], XBC[:, p4, cols], sdc[:, p4].un(2).bc([128, 4, nt]), ALU.mult)
                    k.tt('dve', ybuf[:, :, :nt], ybuf[:, :, :nt], ybanks[q][:, 0:512].re("p (a i) -> p a i", i=nt), ALU.add)
                    k.tt('dve', ZS[:, p4, cols], ybuf[:, :, :nt], ZS[:, p4, cols], ALU.mult)
            else:
                for pr in range(16):
                    k.stt('dve', ytmp[:, :nt], XBC[:, pr, cols], sdc[:, pr:pr + 1], yreg(pr), ALU.mult, ALU.add)
                    k.tt('dve', ZS[:, pr, cols], ytmp[:, :nt], ZS[:, pr, cols], ALU.mult)
            for grp in range(4):
                pn = k.bank()
                yr = yrs[grp % 2]
                for j in range(4):
                    k.act(ysqs[j][:, :nt], ZS[:, 4 * grp + j, cols], AF.Square)
                    k.mm(pn[:, :nt], onesb, ysqs[j][:, :nt], start=(j == 0), stop=(j == 3))
                k.act(yr[:, :nt], pn[:, :nt], AF.Ln, scale=1.0 / 512, bias=epsc[:, 0:1])
                k.act(yr[:, :nt], yr[:, :nt], AF.Exp, scale=-0.5)
                for j in range(4):
                    pr = 4 * grp + j
                    k.stt('dve', ZS[:, pr, cols], ZS[:, pr, cols], snorm[:, pr:pr + 1], yr[:, :nt], ALU.mult, ALU.mult)
            k.release(yidx)
        if tl['last_p']:
            k.dma('sp', o_sSp, sST)
        if DBG['stop'] == 'ssd_mix':
            dumpb(ZS, 16, T)
            dump(sST, 2048, 8192)
            return True
        k.barrier()
        out_proj(1, T, tl, d_swout, ZS)

    tiles = [dict(c0=512 * i, T=512, kind='p', last_p=(i == 3)) for i in range(4)]
    tiles.append(dict(c0=2048, T=64, kind='s', last_p=False))
    if DBG['stop']:
        tiles = [tiles[DBG.get('tile', 0)]]
    def _main(tiles):
        for ti, tl in enumerate(tiles):
            T = tl['T']
            c0 = tl['c0']
            rstate['blk'] = 0
            rstate['first'] = (ti == 0)
            k.dma('sp', X[:, :, :T], d_xT[:, :, c0:c0 + T])
            if gdn(T, tl):
                return k.finish(), hc
            if DBG['stop'] == 'gdn_out':
                dumpf(X, 8, T)
                return k.finish(), hc
            ffn(0, T, tl)
            k.barrier()
            if DBG['stop'] == 'l0':
                dumpf(X, 8, T)
                return k.finish(), hc
            if ssd(T, tl):
                return k.finish(), hc
            if DBG['stop'] == 'ssd_out':
                dumpf(X, 8, T)
                return k.finish(), hc
            ffn(1, T, tl)
            k.barrier()
            A = Arena("fin_")
            Yo = A.get("Yo", [128, 8, 512], F32)
            rstd = rms_rstd(lambda c: X[:, c, :T], 8, T, 1.0 / D)
            for c in range(8):
                k.stt('dve', Yo[:, c, :T], X[:, c, :T], nfin[:, c:c + 1], rstd[:, :T], ALU.mult, ALU.mult)
            k.dma('sp', o_yT[:, :, c0:c0 + T], Yo[:, :, :T])
            k.barrier()
        k.dma('sp', o_ghalo, ghalo)
        k.dma('sp', o_shalo, shalo)
        return k.finish(), hc

    try:
        return _main(tiles)
    except _Stop:
        return k.finish(), hc


def prep_shared(inp):
    f = lambda a: np.ascontiguousarray(a, dtype=np.float32)
    sh = {}
    wm = inp['w_mod']
    sh['wmod'] = np.stack([blk_w(f(wm[l]), 256) for l in range(2)])
    sh['bmod'] = np.ascontiguousarray(np.stack([colvec(f(inp['b_mod'][l])) for l in range(2)], axis=1))
    sh['nmix'] = np.ascontiguousarray(np.stack([colvec(f(inp['norm_mix'][l])) for l in range(2)], axis=1))
    sh['nffn'] = np.ascontiguousarray(np.stack([colvec(f(inp['norm_ffn'][l])) for l in range(2)], axis=1))
    sh['nfin'] = colvec(f(inp['norm_final']))
    gw = f(inp['gdn_w_in'][0])
    sh['gwin'] = blk_w(gw[:, :6144], 256)
    sh['gwba'] = np.ascontiguousarray(gw[:, 6144:6176].reshape(8, 128, 32).transpose(1, 0, 2))
    sh['gcw'] = np.ascontiguousarray(f(inp['gdn_conv_w'][0]).reshape(4, 32, 128).transpose(2, 1, 0))
    sh['galog'] = np.ascontiguousarray(np.broadcast_to(f(inp['gdn_a_log'][0])[None, :], (128, 16)))
    sh['gdtb'] = np.ascontiguousarray(np.broadcast_to(f(inp['gdn_dt_bias'][0])[None, :], (128, 16)))
    sh['gnorm'] = f(inp['gdn_norm'][0]).reshape(128, 1)
    sh['gwout'] = blk_w(f(inp['gdn_w_out'][0]), 128)
    sw = f(inp['ssm_w_in'][0])
    sh['swin'] = blk_w(sw[:, :5120], 256)
    sh['swdt'] = np.ascontiguousarray(sw[:, 5120:5152].reshape(8, 128, 32).transpose(1, 0, 2))
    sh['scw'] = np.ascontiguousarray(f(inp['ssm_conv_w'][0]).reshape(4, 24, 128).transpose(2, 1, 0))
    sh['scb'] = colvec(f(inp['ssm_conv_b'][0]))
    sh['salog'] = np.ascontiguousarray(np.broadcast_to(f(inp['ssm_a_log'][0])[None, :], (128, 32)))
    sh['sdtb'] = np.ascontiguousarray(np.broadcast_to(f(inp['ssm_dt_bias'][0])[None, :], (128, 32)))
    sh['sd'] = colvec(np.repeat(f(inp['ssm_d'][0]), 64))
    sh['snorm'] = colvec(f(inp['ssm_norm'][0]))
    sh['swout'] = blk_w(f(inp['ssm_w_out'][0]), 128)
    fg = inp['ffn_w_gate_up']
    fgu = []
    for l in range(2):
        w = f(fg[l])
        gate, up = w[:, :FFH].reshape(1024, 22, 128), w[:, FFH:].reshape(1024, 22, 128)
        cat = np.concatenate([gate, up], axis=2).reshape(1024, 22 * 256)
        fgu.append(blk_w(cat, 256))
    sh['fgu'] = np.stack(fgu)
    sh['fdn'] = np.stack([blk_w(f(inp['ffn_w_down'][l]), 128) for l in range(2)])
    return sh


def prep_core(inp, i):
    f = lambda a: np.ascontiguousarray(a, dtype=np.float32)
    m = {}
    xs = inp['x_sample'][16 * i:16 * i + 16].reshape(64, D)
    xa = np.concatenate([inp['x_prompt'][i], xs], axis=0)
    m['xT'] = f(xa.T.reshape(8, 128, NTOK).transpose(1, 0, 2))
    ca = np.concatenate([inp['c_prompt'][i:i + 1], inp['c_sample'][16 * i:16 * i + 16]], axis=0)
    m['cT'] = f(ca.T.reshape(8, 128, NSEQ).transpose(1, 0, 2))
    gs = inp['state_gdn'][0, 16 * i:16 * i + 16]
    m['gS0'] = f(gs.transpose(1, 2, 0, 3))
    gh = np.zeros((128, 32, NSEQ, 3), np.float32)
    gc = inp['state_gdn_conv'][0, 16 * i:16 * i + 16]
    gh[:, :, 1:, :] = gc.reshape(16, 3, 32, 128).transpose(3, 2, 0, 1)
    m['ghalo'] = gh
    ss = inp['state_ssm'][0, 16 * i:16 * i + 16]
    m['sS0'] = f(ss.transpose(3, 0, 1, 2).reshape(128, 16, 2048))
    shl = np.zeros((128, 24, NSEQ, 3), np.float32)
    sc = inp['state_ssm_conv'][0, 16 * i:16 * i + 16]
    shl[:, :, 1:, :] = sc.reshape(16, 3, 24, 128).transpose(3, 2, 0, 1)
    m['shalo'] = shl
    return m


def kernel(**inp):
    inp = {k_: np.asarray(v) for k_, v in inp.items()}
    if 'nc' not in _NC_CACHE:
        _NC_CACHE['nc'] = build()
    nc, hc = _NC_CACHE['nc']
    sh = prep_shared(inp)
    in_maps = []
    for i in range(NCORES):
        m = dict(sh)
        m.update(prep_core(inp, i))
        for n, a in hc.items():
            m['c_' + n] = a
        in_maps.append(m)
    res = run_bass_kernel_spmd(nc, in_maps, core_ids=list(range(NCORES)))
    R = res.results
    y_p = np.zeros((8, 2048, D), np.float32)
    y_s = np.zeros((128, 4, D), np.float32)
    g_sp = np.zeros((1, 8, 16, 128, 128), np.float32)
    g_cp = np.zeros((1, 8, 3, 4096), np.float32)
    s_sp = np.zeros((1, 8, 32, 64, 128), np.float32)
    s_cp = np.zeros((1, 8, 3, 3072), np.float32)
    g_ss = np.zeros((1, 128, 16, 128, 128), np.float32)
    g_cs = np.zeros((1, 128, 3, 4096), np.float32)
    s_ss = np.zeros((1, 128, 32, 64, 128), np.float32)
    s_cs = np.zeros((1, 128, 3, 3072), np.float32)
    for i in range(NCORES):
        r = R[i]
        yT = np.asarray(r['yT']).transpose(1, 0, 2).reshape(D, NTOK).T
        y_p[i] = yT[:2048]
        y_s[16 * i:16 * i + 16] = yT[2048:].reshape(16, 4, D)
        g_sp[0, i] = np.asarray(r['gSp']).transpose(1, 0, 2)
        g_ss[0, 16 * i:16 * i + 16] = np.asarray(r['gSs']).transpose(2, 0, 1, 3)
        gh = np.asarray(r['ghalo_o'])
        gflat = gh.transpose(2, 3, 1, 0).reshape(NSEQ, 3, 4096)
        g_cp[0, i] = gflat[0]
        g_cs[0, 16 * i:16 * i + 16] = gflat[1:]
        s_sp[0, i] = np.asarray(r['sSp']).reshape(128, 32, 64).transpose(1, 2, 0)
        s_ss[0, 16 * i:16 * i + 16] = np.asarray(r['sSs']).reshape(128, 16, 32, 64).transpose(1, 2, 3, 0)
        shl = np.asarray(r['shalo_o'])
        sflat = shl.transpose(2, 3, 1, 0).reshape(NSEQ, 3, 3072)
        s_cp[0, i] = sflat[0]
        s_cs[0, 16 * i:16 * i + 16] = sflat[1:]
    return (y_p, y_s, g_sp, g_cp, s_sp, s_cp, g_ss, g_cs, s_ss, s_cs)
```

```python
import numpy as np
from contextlib import ExitStack
import concourse.bass as bass
import concourse.mybir as mybir
from concourse.bass_utils import run_bass_kernel_spmd

F32 = mybir.dt.float32
BF16 = mybir.dt.bfloat16
AF = mybir.ActivationFunctionType
ALU = mybir.AluOpType

NCORES = 8
D = 1024
NTOK = 2112
NSEQ = 17
EPS = 1e-6
NEG = -30000.0
FFH = 2816
ENGS = ('pe', 'act', 'dve', 'pool', 'sp')


class Tile:
    def __init__(self, name):
        self.name = name
        self.w = None
        self.r = {}
        self.dsem = None
        self.dcnt = 0


class V:
    def __init__(self, t, ap):
        self.t = t
        self.ap = ap

    def __getitem__(self, k):
        return V(self.t, self.ap[k])

    def un(self, ax):
        return V(self.t, self.ap.unsqueeze(ax))

    def bc(self, shape):
        return V(self.t, self.ap.to_broadcast(list(shape)))

    def re(self, s, **kw):
        return V(self.t, self.ap.rearrange(s, **kw))

    def bitcast(self, dt):
        return V(self.t, self.ap.bitcast(dt))


class K:
    def __init__(self):
        self.nc = bass.Bass("TRN2", target_bir_lowering=False)
        self.es = ExitStack()
        self.prog = {e: [] for e in ENGS}
        self.cnt = {e: 0 for e in ENGS}
        self.waited = {e: {} for e in ENGS}
        self.esem = {e: self.es.enter_context(self.nc.semaphore("e_" + e)) for e in ENGS}
        self.final = []
        self.dsems = []
        self.dtiles = []
        self.nbank = 0
        self.banks = []
        self.rotl = list(range(8))
        self.held = set()
        self.bidx = {}
        self.uid = 0

    def sb(self, name, shape, dt):
        t = self.es.enter_context(self.nc.sbuf_tensor("s_" + name, list(shape), dt))
        return V(Tile(name), t[:])

    def alias(self, v, name):
        return V(Tile(name), v.ap)

    def dram(self, name, shape, kind, dt=F32):
        return V(None, self.nc.dram_tensor(name, list(shape), dt, kind=kind).ap())

    def mkbanks(self):
        for i in range(8):
            t = self.es.enter_context(self.nc.psum_tensor("bank%d" % i, [128, 512], F32))
            self.banks.append(V(Tile("bank%d" % i), t[:]))
            self.bidx[id(self.banks[-1].t)] = i

    def bank(self, hold=False):
        for _ in range(len(self.rotl)):
            idx = self.rotl[self.nbank % len(self.rotl)]
            self.nbank += 1
            if idx not in self.held:
                break
        else:
            raise AssertionError("no free PSUM bank")
        if hold:
            self.held.add(idx)
        return self.banks[idx]

    def done(self, b):
        self.held.discard(self.bidx[id(b.t)])

    def reserve(self, n):
        got = self.rotl[-n:]
        self.rotl = self.rotl[:-n]
        return [self.banks[i] for i in got], got

    def release(self, idxs):
        self.rotl = self.rotl + list(idxs)

    def _deps(self, e, reads, writes):
        need = {}

        def add(s, v):
            if need.get(s, 0) < v:
                need[s] = v
        for t in reads:
            if t is not None and t.w is not None:
                add(*t.w)
        for t in writes:
            if t is None:
                continue
            if t.w is not None:
                add(*t.w)
            for s, v in t.r.items():
                add(s, v)
        out = []
        for s, v in need.items():
            if e == 'pe' and s is self.esem['pe']:
                continue
            if self.waited[e].get(s, 0) >= v:
                continue
            self.waited[e][s] = v
            out.append((s, v))
        return out

    def op(self, e, fn, reads, writes, inc=True):
        reads = [v.t for v in reads if v is not None and v.t is not None]
        writes = [v.t for v in writes if v is not None and v.t is not None]
        waits = self._deps(e, reads, writes)
        sem = self.esem[e]
        if inc:
            self.cnt[e] += 1
            ev = (sem, self.cnt[e])
        else:
            ev = (sem, self.cnt[e] + 1)
        for t in reads:
            if t.r.get(ev[0], 0) < ev[1]:
                t.r[ev[0]] = ev[1]
        for t in writes:
            t.w = ev
            t.r = {}

        def emit(h):
            for s, v in waits:
                h.wait_ge(s, v)
            ins = fn(h)
            if inc:
                ins.then_inc(sem, 1)
        self.prog[e].append(emit)

    def dma(self, q, out, in_):
        reads = [in_.t] if in_.t is not None else []
        writes = [out.t] if out.t is not None else []
        waits = self._deps(q, reads, writes)
        st = out.t if out.t is not None else in_.t
        if st.dsem is None:
            st.dsem = self.es.enter_context(self.nc.semaphore("d%d" % len(self.dsems)))
            self.dsems.append(st.dsem)
            self.dtiles.append(st)
        st.dcnt += 16
        ev = (st.dsem, st.dcnt)
        for t in reads:
            if t.r.get(ev[0], 0) < ev[1]:
                t.r[ev[0]] = ev[1]
        for t in writes:
            t.w = ev
            t.r = {}
        if out.t is None:
            self.final.append(ev)
        oa, ia, ds = out.ap, in_.ap, st.dsem

        def emit(h):
            for s, v in waits:
                h.wait_ge(s, v)
            h.dma_start(out=oa, in_=ia).then_inc(ds, 16)
        self.prog[q].append(emit)

    def barrier(self):
        evs = [(self.esem[e], self.cnt[e]) for e in ENGS if self.cnt[e] > 0]
        evs += [(t.dsem, t.dcnt) for t in self.dtiles]
        for e in ENGS:
            waits = []
            for s, v in evs:
                if s is self.esem[e]:
                    continue
                if self.waited[e].get(s, 0) >= v:
                    continue
                self.waited[e][s] = v
                waits.append((s, v))

            def emit(h, waits=waits):
                for s, v in waits:
                    h.wait_ge(s, v)
            self.prog[e].append(emit)

    def mm(self, out, lhsT, rhs, start=True, stop=True, inc=None):
        inc = True
        self.op('pe', lambda h: h.matmul(out.ap, lhsT.ap, rhs.ap, start=start, stop=stop),
                [lhsT, rhs], [out], inc=inc)

    def tr(self, out, in_, ident):
        self.op('pe', lambda h: h.transpose(out.ap, in_.ap, ident.ap), [in_, ident], [out])

    def act(self, out, in_, func, bias=None, scale=1.0, e='act'):
        rd = [in_]
        kw = {}
        if isinstance(bias, V):
            rd.append(bias)
            kw['bias'] = bias.ap
        elif bias is not None:
            kw['bias'] = bias
        if isinstance(scale, V):
            rd.append(scale)
            kw['scale'] = scale.ap
        else:
            kw['scale'] = scale
        self.op('act', lambda h: h.activation(out=out.ap, in_=in_.ap, func=func, **kw), rd, [out])

    def tt(self, e, out, in0, in1, op):
        self.op(e, lambda h: h.tensor_tensor(out=out.ap, in0=in0.ap, in1=in1.ap, op=op), [in0, in1], [out])

    def ts(self, e, out, in0, s1, s2=None, op0=ALU.mult, op1=None):
        rd = [in0]
        a1 = s1.ap if isinstance(s1, V) else s1
        a2 = s2.ap if isinstance(s2, V) else s2
        if isinstance(s1, V):
            rd.append(s1)
        if isinstance(s2, V):
            rd.append(s2)
        if op1 is None:
            self.op(e, lambda h: h.tensor_scalar(out=out.ap, in0=in0.ap, scalar1=a1, scalar2=None, op0=op0), rd, [out])
        else:
            self.op(e, lambda h: h.tensor_scalar(out=out.ap, in0=in0.ap, scalar1=a1, scalar2=a2, op0=op0, op1=op1), rd, [out])

    def stt(self, e, out, in0, scalar, in1, op0, op1):
        rd = [in0, in1]
        a = scalar.ap if isinstance(scalar, V) else scalar
        if isinstance(scalar, V):
            rd.append(scalar)
        self.op(e, lambda h: h.scalar_tensor_tensor(out=out.ap, in0=in0.ap, scalar=a, in1=in1.ap, op0=op0, op1=op1), rd, [out])

    def cp(self, e, out, in_):
        if e == 'act':
            self.act(out, in_, AF.Identity)
        else:
            self.op(e, lambda h: h.tensor_copy(out=out.ap, in_=in_.ap), [in_], [out])

    def memset(self, e, out, val):
        self.op(e, lambda h: h.memset(out.ap, val), [], [out])

    def finish(self):
        nc = self.nc
        fin = {}
        for s, v in self.final:
            if fin.get(id(s), (None, 0))[1] < v:
                fin[id(s)] = (s, v)
        finl = list(fin.values())

        def emit(h):
            for s, v in finl:
                h.wait_ge(s, v)
        self.prog['sp'].append(emit)
        with nc.Block() as block:
            @block.tensor
            def _(h):
                for f in self.prog['pe']:
                    f(h)

            @block.scalar
            def _(h):
                for f in self.prog['act']:
                    f(h)

            @block.vector
            def _(h):
                for f in self.prog['dve']:
                    f(h)

            @block.gpsimd
            def _(h):
                for f in self.prog['pool']:
                    f(h)

            @block.sync
            def _(h):
                for f in self.prog['sp']:
                    f(h)
        self.sbuf_left = nc.sbuf_bytes_remaining
        self.es.close()
        return nc


class Var:
    def __init__(self, kind):
        self.kind = kind
        if kind == 'p':
            self.nt, self.nseq, self.tps, self.niter = 128, 1, 128, 6
        else:
            self.nt, self.nseq, self.tps, self.niter = 64, 16, 4, 1


def host_consts():
    c = {}
    c['ident'] = np.eye(128, dtype=np.float32)
    for kind, nt, tps in (('p', 128, 128), ('s', 64, 4)):
        p = np.arange(128)[:, None]
        f = np.arange(128)[None, :]
        same = (p // tps) == (f // tps)
        valid = (p < nt) & (f < nt)
        ge = np.where(same & (f >= p) & valid, 0.0, NEG).astype(np.float32)
        gt = np.where(same & (f > p) & valid, 0.0, NEG).astype(np.float32)
        um = (same & (p <= f) & valid).astype(np.float32)
        c['mge_' + kind] = np.tile(ge, (1, 4))
        c['mgt_' + kind] = np.tile(gt, (1, 4))
        c['um_' + kind] = um
        c['num_' + kind] = -um
        c['same_' + kind] = (same & valid).astype(np.float32)
    p = np.arange(128)[:, None]
    f = np.arange(128)[None, :]
    lv = np.zeros((128, 7, 128), np.float32)
    for l in range(7):
        b = 1 << l
        lv[:, l, :] = ((p > f) & (p // (2 * b) == f // (2 * b)) & (p // b != f // b)).astype(np.float32)
    c['lvl'] = lv
    c['lvlT'] = np.ascontiguousarray(lv[:, 0, :].T)
    tok = np.arange(64)
    selcol = (tok[:, None] // 4 == np.arange(16)[None, :]).astype(np.float32)
    sc = np.zeros((128, 16), np.float32)
    sc[:64] = selcol
    c['selcol'] = sc
    c['seqrow'] = np.broadcast_to(selcol.T[None, :, :], (128, 16, 64)).astype(np.float32).copy()
    return c


def blk_w(w, bc):
    Kd, N = w.shape
    kc = Kd // 128
    return np.ascontiguousarray(w.reshape(kc, 128, N // bc, bc).transpose(2, 1, 0, 3))


def colvec(v):
    return np.ascontiguousarray(v.reshape(-1, 128).T)


_NC_CACHE = {}


DBG = {'stop': None}


class _Stop(Exception):
    pass


def build():
    k = K()
    nc = k.nc
    ein = lambda n, s: k.dram(n, s, "ExternalInput")
    eout = lambda n, s: k.dram(n, s, "ExternalOutput")
    d_xT = ein("xT", [128, 8, NTOK])
    d_cT = ein("cT", [128, 8, NSEQ])
    d_wmod = ein("wmod", [2, 24, 128, 8, 256])
    d_bmod = ein("bmod", [128, 2, 48])
    d_nmix = ein("nmix", [128, 2, 8])
    d_nffn = ein("nffn", [128, 2, 8])
    d_nfin = ein("nfin", [128, 8])
    d_gwin = ein("gwin", [24, 128, 8, 256])
    d_gwba = ein("gwba", [128, 8, 32])
    d_gcw = ein("gcw", [128, 32, 4])
    d_galog = ein("galog", [128, 16])
    d_gdtb = ein("gdtb", [128, 16])
    d_gnorm = ein("gnorm", [128, 1])
    d_gwout = ein("gwout", [8, 128, 16, 128])
    d_swin = ein("swin", [20, 128, 8, 256])
    d_swdt = ein("swdt", [128, 8, 32])
    d_scw = ein("scw", [128, 24, 4])
    d_scb = ein("scb", [128, 24])
    d_salog = ein("salog", [128, 32])
    d_sdtb = ein("sdtb", [128, 32])
    d_sd = ein("sd", [128, 16])
    d_snorm = ein("snorm", [128, 16])
    d_swout = ein("swout", [8, 128, 16, 128])
    d_fgu = ein("fgu", [2, 22, 128, 8, 256])
    d_fdn = ein("fdn", [2, 8, 128, 22, 128])
    d_gS0 = ein("gS0", [16, 128, 16, 128])
    d_ghalo = ein("ghalo", [128, 32, NSEQ, 3])
    d_sS0 = ein("sS0", [128, 16, 2048])
    d_shalo = ein("shalo", [128, 24, NSEQ, 3])
    hc = host_consts()
    d_c = {n: ein("c_" + n, list(a.shape)) for n, a in hc.items()}
    o_yT = eout("yT", [128, 8, NTOK])
    o_gSp = eout("gSp", [128, 16, 128])
    o_gSs = eout("gSs", [16, 128, 16, 128])
    o_ghalo = eout("ghalo_o", [128, 32, NSEQ, 3])
    o_sSp = eout("sSp", [128, 2048])
    o_sSs = eout("sSs", [128, 16, 2048])
    o_shalo = eout("shalo_o", [128, 24, NSEQ, 3])
    o_dbg = eout("dbg", [128, 16384]) if DBG['stop'] else None

    def dump(v, n, off=0):
        k.dma('sp', o_dbg[:, off:off + n], v)

    def stopat(name, items):
        if DBG['stop'] == name:
            off = 0
            for v, n in items:
                dump(v, n, off)
                off += n
            raise _Stop()

    def dumpb(v, nch, T, off=0):
        k.dma('sp', o_dbg[:, off:off + nch * T // 2].re("p (a b) -> p a b", b=T // 2), v[:, 0:nch, :T].bitcast(F32))

    def dumpf(v, nch, T, off=0):
        k.dma('sp', o_dbg[:, off:off + nch * T].re("p (a b) -> p a b", b=T), v[:, 0:nch, :T])

    k.mkbanks()
    ident = k.sb("ident", [128, 128], F32)
    identb = k.sb("identb", [128, 128], BF16)
    onesf = k.sb("onesf", [128, 128], F32)
    onesb = k.sb("onesb", [128, 128], BF16)
    zerosb = k.sb("zerosb", [128, 512], BF16)
    epsc = k.sb("epsc", [128, 1], F32)
    k.dma('sp', ident, d_c['ident'])
    k.cp('dve', identb, ident)
    k.memset('dve', onesf, 1.0)
    k.memset('dve', onesb, 1.0)
    k.memset('dve', zerosb, 0.0)
    k.memset('dve', epsc, EPS)
    CM = {}
    for kind in ('p', 's'):
        for nm in ('mge', 'mgt'):
            t = k.sb(nm + kind, [128, 512], BF16)
            k.dma('pool', t, d_c[nm + '_' + kind])
            CM[nm + kind] = t
        for nm in ('um', 'num', 'same'):
            t = k.sb(nm + kind, [128, 128], F32)
            k.dma('sp', t, d_c[nm + '_' + kind])
            CM[nm + kind] = t
    lvl = k.sb("lvl", [128, 7, 128], BF16)
    k.dma('pool', lvl, d_c['lvl'])
    lvlT = k.sb("lvlT", [128, 128], BF16)
    k.dma('pool', lvlT, d_c['lvlT'])
    selcol = k.sb("selcol", [128, 16], F32)
    k.dma('sp', selcol, d_c['selcol'])
    seqrow = k.sb("seqrow", [128, 16, 64], BF16)
    k.dma('pool', seqrow, d_c['seqrow'])

    def ld(name, dsrc, shape, dt=F32, q='sp'):
        t = k.sb(name, shape, dt)
        k.dma(q, t, dsrc)
        return t
    bmod = ld("bmod", d_bmod, [128, 2, 48])
    nmix = ld("nmix", d_nmix, [128, 2, 8])
    nffn = ld("nffn", d_nffn, [128, 2, 8])
    nfin = ld("nfin", d_nfin, [128, 8])
    gwba = ld("gwba", d_gwba, [128, 8, 32], BF16, 'pool')
    gcw = ld("gcw", d_gcw, [128, 32, 4])
    galog = ld("galog", d_galog, [128, 16])
    gdtb = ld("gdtb", d_gdtb, [128, 16])
    gnorm = ld("gnorm", d_gnorm, [128, 1])
    swdt = ld("swdt", d_swdt, [128, 8, 32], BF16, 'pool')
    scw = ld("scw", d_scw, [128, 24, 4])
    scb = ld("scb", d_scb, [128, 24])
    salog = ld("salog", d_salog, [128, 32])
    sdtb = ld("sdtb", d_sdtb, [128, 32])
    sdc = ld("sdc", d_sd, [128, 16])
    snorm = ld("snorm", d_snorm, [128, 16])
    ghalo = ld("ghalo", d_ghalo, [128, 32, NSEQ, 3])
    shalo = ld("shalo", d_shalo, [128, 24, NSEQ, 3])
    gnegA = k.sb("gnegA", [128, 16], F32)
    snegA = k.sb("snegA", [128, 32], F32)
    k.act(gnegA, galog, AF.Exp)
    k.ts('dve', gnegA, gnegA, -1.0)
    k.act(snegA, salog, AF.Exp)
    k.ts('dve', snegA, snegA, -1.0)

    NSLOT = 4
    ring = [k.sb("wslot%d" % i, [128, 2048], BF16) for i in range(NSLOT)]
    rstate = {'i': 0}

    NBLK = 136
    wscr = k.nc.dram_tensor("wscr", [NBLK, 128, 2048], BF16).ap()
    rstate['blk'] = None
    rstate['first'] = True

    def wload(dsrc, kc, bc):
        s = ring[rstate['i'] % NSLOT]
        rstate['i'] += 1
        n = kc * bc
        v = V(s.t, s.ap[:, 0:n].rearrange("p (k n) -> p k n", n=bc))
        b = rstate['blk']
        if b is None or DBG['stop']:
            k.dma('pool', v, dsrc)
            return v
        rstate['blk'] = b + 1
        flat = V(s.t, s.ap[:, 0:n])
        if rstate['first']:
            k.dma('pool', v, dsrc)
            k.dma('sp', V(None, wscr[b][:, 0:n]), flat)
        else:
            k.dma('sp', flat, V(None, wscr[b][:, 0:n]))
        return v

    X = k.sb("X", [128, 8, 512], F32)
    H = k.sb("H", [128, 8, 512], BF16)
    QKV = k.sb("QKV", [128, 32, 512], BF16)
    ZS = k.sb("ZS", [128, 16, 512], BF16)
    MOD = [k.sb("MOD%d" % l, [128, 48, NSEQ], F32) for l in range(2)]
    csT = k.sb("csT", [128, 8, NSEQ], BF16)
    cTs = k.sb("cTs", [128, 8, NSEQ], F32)
    gSF = k.sb("gSF", [128, 16, 128], F32)
    sST = k.sb("sST", [128, 2048], F32)
    ARENA = k.sb("arena", [128, 33792], BF16)
    cnt = {'sq': 0, 'ta': 0, 'uc': 0, 'acc': 0, 'rs': 0}
    TMP_OFF = 48128

    def rot(lst, key):
        v = lst[cnt[key] % len(lst)]
        cnt[key] += 1
        return v

    class Arena:
        def __init__(self, tag):
            self.off = 0
            self.tag = tag

        def get(self, name, shape, dt):
            n = int(np.prod(shape[1:]))
            nb = n * (4 if dt == F32 else 2)
            nb = (nb + 63) // 64 * 64
            assert self.off + nb <= 33792 * 2, (self.tag, name, self.off, nb)
            ap = ARENA.ap[:, self.off // 2:(self.off + nb) // 2]
            if dt == F32:
                ap = ap.bitcast(F32)
            ap = ap[:, 0:n]
            if len(shape) == 3:
                ap = ap.rearrange("p (a b) -> p a b", b=shape[2])
            elif len(shape) == 4:
                ap = ap.rearrange("p (a b c) -> p a b c", b=shape[2], c=shape[3])
            self.off += nb
            return V(Tile(self.tag + name), ap)

    _ta = Arena("tmp_")
    _ta.off = TMP_OFF
    sqt = [_ta.get("sqt%d" % i, [128, 512], BF16) for i in range(2)]
    tmpA = [_ta.get("tmpA%d" % i, [128, 512], F32) for i in range(2)]
    rstds = [_ta.get("rstd%d" % i, [128, 512], F32) for i in range(2)]
    Uc = [_ta.get("Uc%d" % i, [128, 515], F32) for i in range(2)]
    acc = [_ta.get("acc%d" % i, [128, 512], F32) for i in range(2)]
    k.dma('sp', cTs, d_cT)
    k.act(csT, cTs, AF.Silu)
    k.memset('dve', gSF, 0.0)
    k.memset('dve', sST, 0.0)
    def mod_block(l, b):
        sv = rstate['blk']
        rstate['blk'] = None
        w = wload(d_wmod[l, b], 8, 256)
        rstate['blk'] = sv
        pb = k.bank()
        for c4 in range(2):
            for kc in range(8):
                k.mm(pb[:, c4 * NSEQ:(c4 + 1) * NSEQ], w[:, kc, c4 * 128:(c4 + 1) * 128], csT[:, kc, :],
                     start=(kc == 0), stop=(kc == 7))
        k.tt('dve', MOD[l][:, 2 * b:2 * b + 2, :], pb[:, 0:2 * NSEQ].re("p (a b) -> p a b", b=NSEQ),
             bmod[:, l, 2 * b:2 * b + 2].un(2).bc([128, 2, NSEQ]), ALU.add)

    def mod_gains(l):
        k.stt('dve', MOD[l][:, 8:16, :], MOD[l][:, 8:16, :], 1.0, nmix[:, l, :].un(2).bc([128, 8, NSEQ]), ALU.add, ALU.mult)
        k.stt('dve', MOD[l][:, 32:40, :], MOD[l][:, 32:40, :], 1.0, nffn[:, l, :].un(2).bc([128, 8, NSEQ]), ALU.add, ALU.mult)

    modq = {'b': 0}
    for b in range(24):
        mod_block(0, b)
    mod_gains(0)
    if DBG['stop']:
        for b in range(24):
            mod_block(1, b)
        mod_gains(1)
        modq['b'] = 25
    if DBG['stop'] == 'setup':
        dump(MOD[0].re("p a b -> p (a b)"), 48 * NSEQ)
        dump(MOD[1].re("p a b -> p (a b)"), 48 * NSEQ, 1024)
        return k.finish(), hc
    def seqview(v, T, var):
        if var.kind == 'p':
            return v.un(1)
        return v.re("p (s t) -> p s t", t=4)

    def rms_rstd(src_chunks, nch, T, scale):
        pb = k.bank()
        for c in range(nch):
            sq = rot(sqt, 'sq')
            k.act(sq[:, :T], src_chunks(c), AF.Square)
            k.mm(pb[:, :T], onesb, sq[:, :T], start=(c == 0), stop=(c == nch - 1))
        ta = rot(tmpA, 'ta')
        rstd = rot(rstds, 'rs')
        k.act(ta[:, :T], pb[:, :T], AF.Ln, scale=scale, bias=epsc[:, 0:1])
        k.act(rstd[:, :T], ta[:, :T], AF.Exp, scale=-0.5)
        return rstd

    def normmod(T, tl, gain, shift, out):
        rstd = rms_rstd(lambda c: X[:, c, :T], 8, T, 1.0 / D)
        for c in range(8):
            if tl['kind'] == 'p':
                if shift is None:
                    k.stt('dve', out[:, c, :T], X[:, c, :T], gain[:, c, 0:1], rstd[:, :T], ALU.mult, ALU.mult)
                else:
                    ta = rot(tmpA, 'ta')
                    k.stt('dve', ta[:, :T], X[:, c, :T], gain[:, c, 0:1], rstd[:, :T], ALU.mult, ALU.mult)
                    k.act(out[:, c, :T], ta[:, :T], AF.Identity, bias=shift[:, c, 0:1])
            else:
                ta = rot(tmpA, 'ta')
                k.tt('dve', ta[:, :T], X[:, c, :T], rstd[:, :T], ALU.mult)
                t3 = ta[:, :T].re("p (s t) -> p s t", t=4)
                k.tt('dve', t3, t3, gain[:, c, 1:17].un(2).bc([128, 16, 4]), ALU.mult)
                o3 = out[:, c, :T].re("p (s t) -> p s t", t=4)
                if shift is None:
                    k.cp('dve', o3, t3)
                else:
                    k.tt('dve', o3, t3, shift[:, c, 1:17].un(2).bc([128, 16, 4]), ALU.add)

    def resid_add(T, tl, pb, c, gate):
        if tl['kind'] == 'p':
            k.stt('dve', X[:, c, :T], pb[:, :T], gate[:, c, 0:1], X[:, c, :T], ALU.mult, ALU.add)
        else:
            ta = rot(tmpA, 'ta')
            t3 = ta[:, :T].re("p (s t) -> p s t", t=4)
            k.tt('dve', t3, pb[:, :T].re("p (s t) -> p s t", t=4), gate[:, c, 1:17].un(2).bc([128, 16, 4]), ALU.mult)
            k.tt('dve', X[:, c, :T], X[:, c, :T], ta[:, :T], ALU.add)

    def conv_chunk(T, tl, pb, halo, c, cw, cb, out):
        u = rot(Uc, 'uc')
        a = rot(acc, 'acc')
        if tl['kind'] == 'p':
            nseq, tps, s0 = 1, T, 0
        else:
            nseq, tps, s0 = 16, 4, 1
        w = 3 + tps
        u3 = u[:, 0:nseq * w].re("p (s t) -> p s t", t=w)
        k.cp('dve', u3[:, :, 0:3], halo[:, c, s0:s0 + nseq, :])
        k.act(u3[:, :, 3:w], pb[:, :T].re("p (s t) -> p s t", t=tps), AF.Identity)
        a3 = a[:, :T].re("p (s t) -> p s t", t=tps)
        k.act(a3, u3[:, :, 3:w], AF.Copy, scale=cw[:, c, 3:4])
        for j in (0, 1, 2):
            k.stt('dve', a3, u3[:, :, j:j + tps], cw[:, c, j:j + 1], a3, ALU.mult, ALU.add)
        k.cp('act', halo[:, c, s0:s0 + nseq, :], u3[:, :, tps:tps + 3])

        def fin():
            if cb is None:
                k.act(out[:, c, :T], a[:, :T], AF.Silu)
            else:
                k.act(out[:, c, :T], a[:, :T], AF.Silu, bias=cb[:, c:c + 1])
        return fin

    def softplus_cols(dst, src_ps, bias_bc, nt, n, tmp):
        k.tt('dve', tmp[:nt, :n], src_ps, bias_bc[:nt, :n], ALU.add)
        k.act(tmp[:nt, :n], tmp[:nt, :n], AF.Exp)
        k.act(dst, tmp[:nt, :n], AF.Ln, bias=1.0)

    def ffn(l, T, tl):
        ar = Arena("ffn%d_" % l)
        ACTB = ar.get("act", [128, 22, 512], BF16)
        normmod(T, tl, MOD[l][:, 32:40, :], MOD[l][:, 24:32, :], H)
        for b in range(22):
            w = wload(d_fgu[l, b], 8, 256)
            pg = k.bank()
            pu = k.bank()
            for kc in range(8):
                k.mm(pg[:, :T], w[:, kc, 0:128], H[:, kc, :T], start=(kc == 0), stop=(kc == 7))
            for kc in range(8):
                k.mm(pu[:, :T], w[:, kc, 128:256], H[:, kc, :T], start=(kc == 0), stop=(kc == 7))
            sg = rot(sqt, 'sq')
            k.act(sg[:, :T], pg[:, :T], AF.Silu)
            k.tt('dve', ACTB[:, b, :T], pu[:, :T], sg[:, :T], ALU.mult)
        for dch in range(8):
            pb = k.bank()
            for hf in range(2):
                w = wload(d_fdn[l, dch][:, 11 * hf:11 * hf + 11, :], 11, 128)
                for kk in range(11):
                    kc = 11 * hf + kk
                    k.mm(pb[:, :T], w[:, kk, :], ACTB[:, kc, :T], start=(kc == 0), stop=(kc == 21))
            resid_add(T, tl, pb, dch, MOD[l][:, 40:48, :])

    def out_proj(l, T, tl, dw, src):
        for b in range(8):
            w = wload(dw[b], 16, 128)
            pb = k.bank()
            for kc in range(16):
                k.mm(pb[:, :T], w[:, kc, :], src[:, kc, :T], start=(kc == 0), stop=(kc == 15))
            resid_add(T, tl, pb, b, MOD[l][:, 16:24, :])

    def pipeline(gens, depth):
        active = []
        it = iter(gens)
        done = False
        while True:
            if len(active) < depth and not done:
                g_ = next(it, None)
                if g_ is None:
                    done = True
                else:
                    active.append(g_)
            if not active:
                if done:
                    break
                continue
            for g_ in list(active):
                try:
                    next(g_)
                except StopIteration:
                    active.remove(g_)

    def exp_args(ar, var, gcols, lbcols, want_e):
        nt = var.nt
        kd = var.kind
        RXc = ar['RXc']
        gb = gcols.un(2).bc([nt, 4, nt])
        k.tt('pool', RXc[:nt, :, :nt], gb, CM['um' + kd][:nt, :nt].un(1).bc([nt, 4, nt]), ALU.mult)
        if want_e:
            RXe = ar['RXe']
            k.tt('pool', RXe[:nt, :, :nt], lbcols.un(2).bc([nt, 4, nt]), ident[:nt, :nt].un(1).bc([nt, 4, nt]), ALU.mult)
        yield
        N = 4 * nt
        r2 = lambda v: v[:nt, :, :nt]
        pd = k.bank(hold=True)
        pdv = pd[:nt, 0:N].re("p (h i) -> p h i", i=nt)
        k.mm(pdv, onesf[:nt, :nt], r2(RXc), start=True, stop=False)
        k.mm(pdv, CM['num' + kd][:nt, :nt], gb, start=False, stop=False)
        k.mm(pdv, identb[:nt, :nt], CM['mge' + kd][:nt, 0:512].re("p (h i) -> p h i", i=128)[:, :, :nt], start=False, stop=True)
        pe_ = k.bank(hold=True)
        pev = pe_[:, 0:N].re("p (h i) -> p h i", i=nt)
        k.mm(pev, onesf[:nt, :], r2(RXc), start=True, stop=True)
        if want_e:
            pq = k.bank(hold=True)
            pqv = pq[:nt, 0:N].re("p (h i) -> p h i", i=nt)
            k.mm(pqv, onesf[:nt, :nt], r2(RXc), start=True, stop=False)
            k.mm(pqv, onesf[:nt, :nt], r2(RXe), start=False, stop=False)
            k.mm(pqv, CM['num' + kd][:nt, :nt], gb, start=False, stop=False)
            k.mm(pqv, identb[:nt, :nt], CM['mgt' + kd][:nt, 0:512].re("p (h i) -> p h i", i=128)[:, :, :nt], start=False, stop=True)
        yield
        k.act(ar['DT'][:nt, :, :nt], pdv, AF.Exp)
        k.act(ar['EB'][:, :, :nt], pev, AF.Exp)
        k.done(pd)
        k.done(pe_)
        if want_e:
            k.act(ar['E'][:nt, :, :nt], pqv, AF.Exp)
            k.done(pq)
        yield

    def gdn(T, tl):
        l = 0
        normmod(T, tl, MOD[l][:, 8:16, :], MOD[l][:, 0:8, :], H)
        if DBG['stop'] == 'norm0':
            dumpb(H, 8, T)
            return True
        pendf = [None]
        for b in range(24):
            w = wload(d_gwin[b], 8, 256)
            for j in range(2):
                ch = 2 * b + j
                pb = k.bank()
                for kc in range(8):
                    k.mm(pb[:, :T], w[:, kc, j * 128:(j + 1) * 128], H[:, kc, :T], start=(kc == 0), stop=(kc == 7))
                if ch < 32:
                    fin = conv_chunk(T, tl, pb, ghalo, ch, gcw, None, QKV)
                    if pendf[0] is not None:
                        pendf[0]()
                    pendf[0] = fin
                else:
                    if pendf[0] is not None:
                        pendf[0]()
                        pendf[0] = None
                    k.act(ZS[:, ch - 32, :T], pb[:, :T], AF.Silu)
        pend = None
        for c2 in range(17):
            cur = None
            if c2 < 16:
                pb = k.bank(hold=True)
                sq = rot(sqt, 'sq')
                k.act(sq[:, :T], QKV[:, c2, :T], AF.Square)
                k.mm(pb[:, :T], onesb, sq[:, :T], start=True, stop=True)
                cur = (c2, pb)
            if pend is not None:
                c1, pb1 = pend
                ta = rot(tmpA, 'ta')
                rstd = rot(rstds, 'rs')
                k.act(ta[:, :T], pb1[:, :T], AF.Ln, scale=1.0, bias=epsc[:, 0:1])
                k.done(pb1)
                k.act(rstd[:, :T], ta[:, :T], AF.Exp, scale=-0.5)
                if c1 < 8:
                    k.stt('dve', QKV[:, c1, :T], QKV[:, c1, :T], 128.0 ** -0.5, rstd[:, :T], ALU.mult, ALU.mult)
                else:
                    k.tt('dve', QKV[:, c1, :T], QKV[:, c1, :T], rstd[:, :T], ALU.mult)
            pend = cur
        if DBG['stop'] == 'gdn_in':
            dumpb(QKV, 32, T)
            dumpb(ZS, 4, T, 8192)
            return True
        k.barrier()
        var = Var(tl['kind'])
        nt = var.nt
        kd = var.kind
        A = Arena("gdn_")
        NCTX = 3 if kd == 'p' else 1
        ctxs = []
        shRX = [dict(RXc=A.get('RXc_%d' % i, [128, 4, 128], F32), RXe=A.get('RXe_%d' % i, [128, 4, 128], F32))
                for i in range(1)]
        for ci_ in range(NCTX):
            c_ = dict(shRX[ci_ % len(shRX)])
            for nm in ('EB',):
                c_[nm] = A.get('%s_%d' % (nm, ci_), [128, 4, 128], F32)
            for nm in ('DT', 'E', 'T1', 'Yn', 'Tb0', 'Tb1', 'Wb0', 'Wb1', 'Rv', 'Rk', 'Kd', 'QD', 'QK'):
                c_[nm] = A.get('%s_%d' % (nm, ci_), [128, 4, 128], BF16)
            ctxs.append(c_)
        NHB = 2 if kd == 'p' else 1
        for ci_, c_ in enumerate(ctxs):
            hbl = []
            for i in range(NHB):
                ot = A.get('otmp%d_%d' % (i, ci_), [128, 128], F32)
                hbl.append(dict(nwT=A.get('nwT%d_%d' % (i, ci_), [128, 128], BF16),
                                delta=A.get('delta%d_%d' % (i, ci_), [128, 128], BF16),
                                osq=A.get('osq%d_%d' % (i, ci_), [128, 128], BF16), otmp=ot, orstd=ot,
                                otmp2=A.get('otmp2%d_%d' % (i, ci_), [128, 128], F32)))
            c_['hb'] = hbl
        CSb = [A.get('CS%d' % i, [128, 10, 16], F32) for i in range(2)]
        CS = CSb[0]
        gSB = A.get('gSB', [128, 16, 128], BF16)
        tmpc = A.get('tmpc', [128, 32], F32)
        if kd == 's':
            nwTz = A.get('nwTz', [128, 16, 64], BF16)
            kdz = A.get('kdz', [128, 16, 128], BF16)
            SFsb = [A.get('SFs%d' % i, [128, 16, 128], F32) for i in range(2)]
            SBsb = [A.get('SBs%d' % i, [128, 16, 128], BF16) for i in range(2)]
            gpf = {'n': 0}

            def gload(h):
                if h == gpf['n'] and h < 16:
                    k.dma('sp', SFsb[h % 2], d_gS0[h])
                    gpf['n'] += 1
        nch = DBG.get('nch', T // nt)
        gdone = {}
        cast_done = {}
        ctx_busy = {}

        def chunk_stats(ci):
            cols = slice(ci * nt, (ci + 1) * nt)
            CS = CSb[ci % 2]
            c_nl, c_lb, c_beta, c_g, c_cum, c_tot, c_bec, c_kdec, c_t1 = [CS[:, i, :] for i in range(9)]
            pg = k.bank()
            for kc in range(8):
                k.mm(pg[:nt, 0:32], H[:, kc, cols], gwba[:, kc, :], start=(kc == 0), stop=(kc == 7))
            k.act(tmpc[:nt, 0:16], pg[:nt, 0:16], AF.Exp, scale=-1.0)
            k.act(c_nl[:nt], tmpc[:nt, 0:16], AF.Ln, bias=1.0)
            k.ts('dve', c_lb[:nt], c_nl[:nt], -1.0)
            k.act(c_beta[:nt], c_nl[:nt], AF.Exp, scale=-1.0)
            softplus_cols(c_t1[:nt], pg[:nt, 16:32], gdtb, nt, 16, tmpc[:, 16:32])
            k.tt('dve', c_g[:nt], c_t1[:nt], gnegA[:nt], ALU.mult)
            pc = k.bank()
            k.mm(pc[:nt, 0:16], CM['um' + kd][:nt, :nt], c_g[:nt], start=True, stop=True)
            k.mm(pc[:nt, 16:32], CM['same' + kd][:nt, :nt], c_g[:nt], start=True, stop=True)
            k.cp('dve', c_cum[:nt], pc[:nt, 0:16])
            k.cp('dve', c_tot[:nt], pc[:nt, 16:32])
            k.act(c_t1[:nt], c_cum[:nt], AF.Exp)
            k.tt('dve', c_bec[:nt], c_t1[:nt], c_beta[:nt], ALU.mult)
            k.tt('dve', c_t1[:nt], c_tot[:nt], c_cum[:nt], ALU.subtract)
            k.act(c_kdec[:nt], c_t1[:nt], AF.Exp)
            stopat('g_a', [(CS.re("p a b -> p (a b)"), 160)])
            yield

        if True:
            def group(g, cx, ci):
                cols = slice(ci * nt, (ci + 1) * nt)
                CS = CSb[ci % 2]
                c_nl, c_lb, c_beta, c_g, c_cum, c_tot, c_bec, c_kdec, c_t1 = [CS[:, i, :] for i in range(9)]
                assert not ctx_busy.get(id(cx), False)
                ctx_busy[id(cx)] = True
                if DBG.get('frontwait'):
                    while ci > 0 and gdone.get(ci - 1, 0) < ng_:
                        yield
                hs = slice(4 * g, 4 * g + 4)
                for _ in range(2):
                    if modq['b'] < 24:
                        mod_block(1, modq['b'])
                        modq['b'] += 1
                for _ in exp_args(cx, var, c_g[:nt, hs], c_lb[:nt, hs], True):
                    yield
                DT, E, EB = cx['DT'], cx['E'], cx['EB']
                Rv, Rk, Kd, QD, QK = cx['Rv'], cx['Rk'], cx['Kd'], cx['QD'], cx['QK']
                Tb = [cx['Tb0'], cx['Tb1']]
                Wb = [cx['Wb0'], cx['Wb1']]
                pG = k.bank()
                pQ = k.bank()
                for e in range(2):
                    qh = 2 * g + e
                    kT = QKV[:, 8 + qh, cols]
                    qT = QKV[:, qh, cols]
                    k.mm(pG[:nt, e * nt:(e + 1) * nt], kT, kT, start=True, stop=True)
                    k.mm(pQ[:nt, e * nt:(e + 1) * nt], kT, qT, start=True, stop=True)
                T1 = cx['T1']
                for e in range(2):
                    Gv = pG[:nt, e * nt:(e + 1) * nt].un(1).bc([nt, 2, nt])
                    k.stt('dve', T1[:nt, 2 * e:2 * e + 2, :nt], Gv, -1.0, E[:nt, 2 * e:2 * e + 2, :nt], ALU.mult, ALU.mult)
                    Qv = pQ[:nt, e * nt:(e + 1) * nt].un(1).bc([nt, 2, nt])
                    k.tt('dve', QK[:nt, 2 * e:2 * e + 2, :nt], Qv, DT[:nt, 2 * e:2 * e + 2, :nt], ALU.mult)
                yield
                Yn = cx['Yn']
                Wc, Wtc = Wb[1], Tb[1]
                pT = k.bank(hold=True)
                pTb = pT.bitcast(BF16)
                for hl in range(4):
                    k.tr(pTb[:nt, hl * nt:(hl + 1) * nt], T1[:nt, hl, :nt], identb[:nt, :nt])
                k.tt('pool', Wtc[:nt, :, :nt], T1[:nt, :, :nt], lvlT[:nt, 0:nt].un(1).bc([nt, 4, nt]), ALU.mult)
                k.tt('pool', Wtc[:nt, :, :nt], Wtc[:nt, :, :nt], identb[:nt, :nt].un(1).bc([nt, 4, nt]), ALU.add)
                yield
                k.tt('dve', Wc[:nt, :, :nt], pTb[:nt, 0:4 * nt].re("p (h i) -> p h i", i=nt),
                     lvl[:nt, 0, :nt].un(1).bc([nt, 4, nt]), ALU.mult)
                k.done(pT)
                k.tt('pool', Wc[:nt, :, :nt], Wc[:nt, :, :nt], identb[:nt, :nt].un(1).bc([nt, 4, nt]), ALU.add)
                yield
                nlev = 7 if kd == 'p' else 2
                for lev in range(1, nlev):
                    Wn, Wtn = Wb[lev % 2], Tb[lev % 2]
                    pY = k.bank(hold=True)
                    for hl in range(4):
                        k.mm(pY[:nt, hl * nt:(hl + 1) * nt], T1[:nt, hl, :nt], Wc[:nt, hl, :nt], start=True, stop=True)
                    yield
                    k.tt('dve', Yn[:nt, :, :nt], pY[:nt, 0:4 * nt].re("p (h i) -> p h i", i=nt),
                         lvl[:nt, lev, :nt].un(1).bc([nt, 4, nt]), ALU.mult)
                    k.done(pY)
                    lastl = (lev == nlev - 1)
                    if not lastl:
                        pW = k.bank(hold=True)
                        for hl in range(4):
                            k.mm(pW[:nt, hl * nt:(hl + 1) * nt], Wtc[:nt, hl, :nt], Yn[:nt, hl, :nt], start=True, stop=True)
                    pWt = k.bank(hold=True)
                    for hl in range(4):
                        k.mm(pWt[:nt, hl * nt:(hl + 1) * nt], Yn[:nt, hl, :nt], Wtc[:nt, hl, :nt], start=True, stop=True)
                    yield
                    if not lastl:
                        k.tt('dve', Wn[:nt, :, :nt], pW[:nt, 0:4 * nt].re("p (h i) -> p h i", i=nt), Wc[:nt, :, :nt], ALU.add)
                        k.done(pW)
                    k.tt('dve', Wtn[:nt, :, :nt], pWt[:nt, 0:4 * nt].re("p (h i) -> p h i", i=nt), Wtc[:nt, :, :nt], ALU.add)
                    k.done(pWt)
                    Wc, Wtc = Wn, Wtn
                    yield
                Wc = Wtc
                pv = k.bank(hold=True)
                pvb = pv.bitcast(BF16)
                for hl in range(4):
                    k.tr(pvb[:nt, hl * 128:(hl + 1) * 128], QKV[:, 16 + 4 * g + hl, cols], identb)
                for e in range(2):
                    k.tr(pvb[:nt, 512 + e * 128:512 + (e + 1) * 128], QKV[:, 8 + 2 * g + e, cols], identb)
                for e in range(2):
                    qv = QKV[:, 2 * g + e, cols].un(1).bc([128, 2, nt])
                    k.tt('pool', QD[:, 2 * e:2 * e + 2, :nt], qv, EB[:, 2 * e:2 * e + 2, :nt], ALU.mult)
                yield
                k.tt('dve', Rv[:nt], pvb[:nt, 0:512].re("p (h v) -> p h v", v=128),
                     c_beta[:nt, hs].un(2).bc([nt, 4, 128]), ALU.mult)
                for e in range(2):
                    kv = pvb[:nt, 512 + e * 128:512 + (e + 1) * 128].un(1).bc([nt, 2, 128])
                    h2 = slice(4 * g + 2 * e, 4 * g + 2 * e + 2)
                    k.tt('dve', Rk[:nt, 2 * e:2 * e + 2, :], kv, c_bec[:nt, h2].un(2).bc([nt, 2, 128]), ALU.mult)
                    k.tt('dve', Kd[:nt, 2 * e:2 * e + 2, :], kv, c_kdec[:nt, h2].un(2).bc([nt, 2, 128]), ALU.mult)
                k.done(pv)
                yield
                if kd == 'p':
                    while ci > 0 and gdone.get(ci - 1, 0) < ng_:
                        yield
                    if g == 0:
                        k.cp('act', gSB, gSF)
                        cast_done[ci] = True
                    while not cast_done.get(ci, False):
                        yield
                for hl in range(4):
                    h = 4 * g + hl
                    hbs = cx['hb'][hl % NHB]
                    nwT, delta, osq, otmp, orstd, otmp2 = (hbs['nwT'], hbs['delta'], hbs['osq'], hbs['otmp'],
                                                           hbs['orstd'], hbs['otmp2'])
                    if kd == 's':
                        gload(h)
                        gload(h + 1)
                        SFs, SBs = SFsb[h % 2], SBsb[h % 2]
                        k.cp('act', SBs, SFs)
                        Sf = lambda s, SFs=SFs: SFs[:, s, :]
                        Sb = lambda s, SBs=SBs: SBs[:, s, :]
                    else:
                        Sf = lambda s, h=h: gSF[:, h, :]
                        Sb = lambda s, h=h: gSB[:, h, :]
                    pw = k.bank()
                    k.mm(pw[:, :nt], Rk[:nt, hl, :], Wc[:nt, hl, :nt], start=True, stop=True)
                    k.act(nwT[:, :nt], pw[:, :nt], AF.Identity, scale=-1.0)
                    if kd == 's':
                        k.tt('dve', nwTz[:, :, :nt], nwT[:, :nt].un(1).bc([128, 16, nt]), seqrow[:, :, :nt], ALU.mult)
                        k.tt('dve', kdz[:nt], Kd[:nt, hl, :].un(1).bc([nt, 16, 128]),
                             selcol[:nt, :].un(2).bc([nt, 16, 128]), ALU.mult)
                    pdl = k.bank()
                    k.mm(pdl[:nt, 0:128], Wc[:nt, hl, :nt], Rv[:nt, hl, :], start=True, stop=False)
                    for s in range(var.nseq):
                        lw = nwTz[:, s, :nt] if kd == 's' else nwT[:, :nt]
                        k.mm(pdl[:nt, 0:128], lw, Sb(s), start=False, stop=(s == var.nseq - 1))
                    k.cp('act', delta[:nt], pdl[:nt, 0:128])
                    yield
                    po = k.bank()
                    k.mm(po[:, :nt], delta[:nt], QK[:nt, hl, :nt], start=True, stop=False)
                    for s in range(var.nseq):
                        cs = slice(s * var.tps, (s + 1) * var.tps)
                        k.mm(po[:, cs], Sb(s), QD[:, hl, cs], start=False, stop=(s == var.nseq - 1))
                    k.act(osq[:, :nt], po[:, :nt], AF.Square)
                    pn = k.bank()
                    k.mm(pn[:, :nt], onesb, osq[:, :nt], start=True, stop=True)
                    k.act(otmp[:, :nt], pn[:, :nt], AF.Ln, scale=1.0 / 128, bias=epsc[:, 0:1])
                    k.act(orstd[:, :nt], otmp[:, :nt], AF.Exp, scale=-0.5)
                    k.tt('dve', otmp2[:, :nt], po[:, :nt], orstd[:, :nt], ALU.mult)
                    k.stt('dve', ZS[:, h, cols], otmp2[:, :nt], gnorm[:, 0:1], ZS[:, h, cols], ALU.mult, ALU.mult)
                    for s in range(var.nseq):
                        ps = k.bank()
                        lk = kdz[:nt, s, :] if kd == 's' else Kd[:nt, hl, :]
                        k.mm(ps[:, 0:128], lk, delta[:nt], start=True, stop=True)
                        lastc = (s + 1) * var.tps - 1
                        k.stt('dve', Sf(s), Sf(s), EB[:, hl, lastc:lastc + 1], ps[:, 0:128], ALU.mult, ALU.add)
                    if kd == 's':
                        k.dma('sp', o_gSs[h], SFs)
                    yield
                ctx_busy[id(cx)] = False
                gdone[ci] = gdone.get(ci, 0) + 1

            ng_ = DBG.get('ng', 4)
            gens = []
            for ci in range(nch):
                gens.append(chunk_stats(ci))
                for g in range(ng_):
                    gi = ci * ng_ + g
                    gens.append(group(g, ctxs[gi % NCTX], ci))
            pipeline(gens, NCTX)
        while modq['b'] < 24:
            mod_block(1, modq['b'])
            modq['b'] += 1
        if modq['b'] == 24:
            mod_gains(1)
            modq['b'] = 25
        if tl['last_p']:
            k.dma('sp', o_gSp, gSF)
        if DBG['stop'] == 'gdn_mix':
            dumpb(ZS, 16, T)
            dump(gSF.re("p a b -> p (a b)"), 2048, 8192)
            return True
        k.barrier()
        out_proj(0, T, tl, d_gwout, ZS)

    def ssd(T, tl):
        l = 1
        normmod(T, tl, MOD[l][:, 8:16, :], MOD[l][:, 0:8, :], H)
        XBC = QKV
        pendf = [None]
        for b in range(20):
            w = wload(d_swin[b], 8, 256)
            for j in range(2):
                ch = 2 * b + j
                pb = k.bank()
                for kc in range(8):
                    k.mm(pb[:, :T], w[:, kc, j * 128:(j + 1) * 128], H[:, kc, :T], start=(kc == 0), stop=(kc == 7))
                if ch < 16:
                    k.act(ZS[:, ch, :T], pb[:, :T], AF.Silu)
                else:
                    fin = conv_chunk(T, tl, pb, shalo, ch - 16, scw, scb, XBC)
                    if pendf[0] is not None:
                        pendf[0]()
                    pendf[0] = fin
        if pendf[0] is not None:
            pendf[0]()
            pendf[0] = None
        if DBG['stop'] == 'ssd_in':
            dumpb(QKV, 24, T)
            dumpb(ZS, 4, T, 8192)
            return True
        k.barrier()
        var = Var(tl['kind'])
        nt = var.nt
        kd = var.kind
        A = Arena("ssd_")
        ars = []
        for i_ in range(2):
            ars.append(dict(RXc=A.get('RXc%d' % i_, [128, 4, 128], F32), EB=A.get('EB%d' % i_, [128, 4, 128], F32),
                            DT=A.get('DT%d' % i_, [128, 4, 128], BF16)))
        MT = A.get('MT', [128, 32, nt], BF16)
        CE = A.get('CE', [128, 32, nt], BF16)
        XDTz = A.get('XDTz', [128, 16, 2, 128], BF16)
        XDTD = A.get('XDTD', [128, 2048], BF16)
        BTM = A.get('BTM', [128, 4, 128], BF16)
        CBs = A.get('CBs', [128, 4, 128], BF16)
        STz = A.get('STz', [128, 32, 128], BF16)
        CS = A.get('CS', [128, 8, 32], F32)
        EL = A.get('EL', [128, 32, 16], F32)
        ytmp = A.get('ytmp', [128, 128], F32)
        ybuf = A.get('ybuf', [128, 4, 128], F32) if kd == 'p' else None
        ysqs = [A.get('ysq%d' % i, [128, 128], BF16) for i in range(4)]
        yrs = [A.get('yrs%d' % i, [128, 128], F32) for i in range(2)]
        if kd == 's':
            STsb = [A.get('STs%d' % i, [128, 2048], F32) for i in range(2)]
            spf = {'n': 0}

            def sload(s_):
                if s_ == spf['n'] and s_ < 16:
                    k.dma('sp', STsb[s_ % 2], d_sS0[:, s_, :])
                    spf['n'] += 1
            XDz = A.get('XDz', [128, 2048], BF16)
        tmpc = A.get('tmpc', [128, 32], F32)
        c_dt, c_la, c_cum, c_tot, c_dtd, c_t1 = [CS[:, i, :] for i in range(6)]
        k.memset('dve', XDTz, 0.0)
        k.memset('dve', STz, 0.0)
        nch = T // nt
        for ci in range(nch):
            cols = slice(ci * nt, (ci + 1) * nt)
            pg = k.bank()
            for kc in range(8):
                k.mm(pg[:nt, 0:32], H[:, kc, cols], swdt[:, kc, :], start=(kc == 0), stop=(kc == 7))
            softplus_cols(c_dt[:nt], pg[:nt, 0:32], sdtb, nt, 32, tmpc)
            k.tt('dve', c_la[:nt], c_dt[:nt], snegA[:nt], ALU.mult)
            pc = k.bank()
            k.mm(pc[:nt, 0:32], CM['um' + kd][:nt, :nt], c_la[:nt], start=True, stop=True)
            k.mm(pc[:nt, 32:64], CM['same' + kd][:nt, :nt], c_la[:nt], start=True, stop=True)
            k.cp('dve', c_cum[:nt], pc[:nt, 0:32])
            k.cp('dve', c_tot[:nt], pc[:nt, 32:64])
            k.tt('dve', c_t1[:nt], c_tot[:nt], c_cum[:nt], ALU.subtract)
            k.act(c_t1[:nt], c_t1[:nt], AF.Exp)
            k.tt('dve', c_dtd[:nt], c_t1[:nt], c_dt[:nt], ALU.mult)
            pCB = k.bank()
            for g in range(4):
                k.mm(pCB[:nt, g * nt:(g + 1) * nt], XBC[:, 16 + g, cols], XBC[:, 20 + g, cols], start=True, stop=True)
            k.cp('act', CBs[:nt, :, :nt], pCB[:nt, 0:4 * nt].re("p (g i) -> p g i", i=nt))
            def sgroup(g8, ar, cols=cols):
                hs = slice(4 * g8, 4 * g8 + 4)
                for _ in exp_args(ar, var, c_la[:nt, hs], None, False):
                    yield
                grp = g8 // 2
                k.tt('dve', MT[:nt, hs, :nt], CBs[:nt, grp, :nt].un(1).bc([nt, 4, nt]),
                     ar['DT'][:nt, :, :nt], ALU.mult)
                k.tt('dve', CE[:, hs, :nt], XBC[:, 20 + grp, cols].un(1).bc([128, 4, nt]), ar['EB'][:, :, :nt], ALU.mult)
                k.cp('dve', EL[:, hs, 0:var.nseq], ar['EB'][:, :, var.tps - 1:nt:var.tps])
                yield
                if g8 < 4:
                    q4 = g8
                    px = k.bank(hold=True)
                    pxb = px.bitcast(BF16)
                    for j in range(4):
                        k.tr(pxb[:nt, j * 128:(j + 1) * 128], XBC[:, 4 * q4 + j, cols], identb)
                    yield
                    xv = pxb[:nt, 0:512].re("p (h d) -> p h d", d=64)
                    h8 = slice(8 * q4, 8 * q4 + 8)
                    xv4 = pxb[:nt, 0:512].re("p (a e d) -> p a e d", e=2, d=64)
                    for e in range(2):
                        k.tt('dve', XDTz[:nt, 4 * q4:4 * q4 + 4, e, e * 64:(e + 1) * 64], xv4[:, :, e, :],
                             c_dt[:nt, 8 * q4 + e:8 * q4 + 8:2].un(2).bc([nt, 4, 64]), ALU.mult)
                    k.tt('dve', XDTD[:nt, 512 * q4:512 * (q4 + 1)].re("p (h d) -> p h d", d=64), xv,
                         c_dtd[:nt, h8].un(2).bc([nt, 8, 64]), ALU.mult)
                    k.done(px)
                    yield
                elif g8 == 4:
                    pbm = k.bank(hold=True)
                    pbb = pbm.bitcast(BF16)
                    for g in range(4):
                        k.tr(pbb[:nt, g * 128:(g + 1) * 128], XBC[:, 16 + g, cols], identb)
                    yield
                    k.cp('act', BTM[:nt], pbb[:nt, 0:512].re("p (g n) -> p g n", n=128))
                    k.done(pbm)
                    yield

            pipeline([sgroup(g8, ars[g8 % 2]) for g8 in range(8)], 2)
            ybanks, yidx = k.reserve((16 * nt * 4 + 2047) // 2048)
            per = 512 // nt

            def yreg(pr):
                return ybanks[pr // per][:, (pr % per) * nt:(pr % per + 1) * nt]
            for yb in ybanks:
                k.mm(yb[:, 0:512], zerosb[:, 0:128], zerosb[:, 0:512], start=True, stop=False)
            for pr in range(16):
                for e in range(2):
                    k.mm(yreg(pr), XDTz[:nt, pr, e, :], MT[:nt, 2 * pr + e, :nt], start=False, stop=False)
            for s in range(var.nseq):
                if kd == 's':
                    sload(s)
                    sload(s + 1)
                    STs = STsb[s % 2]
                    ST = STs
                else:
                    ST = sST
                STz4 = STz.re("p (a e) c -> p a e c", e=2)
                ST4 = ST.re("p (a e d) -> p a e d", e=2, d=64)
                k.cp('act', STz4[:, :, 0, 0:64], ST4[:, :, 0, :])
                k.cp('dve', STz4[:, :, 1, 64:128], ST4[:, :, 1, :])
                cs = slice(s * var.tps, (s + 1) * var.tps)
                for pr in range(16):
                    for e in range(2):
                        lastmm = (s == var.nseq - 1 and e == 1 and (pr % per == per - 1 or pr == 15))
                        reg = yreg(pr)
                        k.mm(reg[:, cs], STz[:, 2 * pr + e, :], CE[:, 2 * pr + e, cs], start=False, stop=lastmm)
                if kd == 's':
                    k.ts('dve', XDz[:nt], XDTD[:nt], selcol[:nt, s:s + 1])
                    xd = XDz
                else:
                    xd = XDTD
                for g in range(4):
                    pu = k.bank()
                    k.mm(pu[:, 0:512], BTM[:nt, g, :], xd[:nt, 512 * g:512 * (g + 1)], start=True, stop=True)
                    stv = ST[:, 512 * g:512 * (g + 1)].re("p (h d) -> p h d", d=64)
                    k.tt('dve', stv, stv, EL[:, 8 * g:8 * g + 8, s:s + 1].bc([128, 8, 64]), ALU.mult)
                    k.tt('dve', ST[:, 512 * g:512 * (g + 1)], ST[:, 512 * g:512 * (g + 1)], pu[:, 0:512], ALU.add)
                if kd == 's':
                    k.dma('sp', o_sSs[:, s, :], STs)
            if DBG['stop'] == 'ssd_y' and ci == DBG.get('ci', 0):
                for pr in range(16):
                    k.cp('dve', ytmp[:, :nt], yreg(pr))
                    dump(ytmp[:, :nt], nt, pr * nt)
                dump(CS[:, 0:6, :].re("p a b -> p (a b)"), 192, 16 * nt)
                raise _Stop()
            if kd == 'p':
                for q in range(4):
                    p4 = slice(4 * q, 4 * q + 4)
                    k.tt('pool', ybuf[:, :, :nt], XBC[:, p4, cols], sdc[:, p4].un(2).bc([128, 4, nt]), ALU.mult)
                    k.tt('dve', ybuf[:, :, :nt], ybuf[:, :, :nt], ybanks[q][:, 0:512].re("p (a i) -> p a i", i=nt), ALU.add)
                    k.tt('dve', ZS[:, p4, cols], ybuf[:, :, :nt], ZS[:, p4, cols], ALU.mult)
            else:
                for pr in range(16):
                    k.stt('dve', ytmp[:, :nt], XBC[:, pr, cols], sdc[:, pr:pr + 1], yreg(pr), ALU.mult, ALU.add)
                    k.tt('dve', ZS[:, pr, cols], ytmp[:, :nt], ZS[:, pr, cols], ALU.mult)
            for grp in range(4):
                pn = k.bank()
                yr = yrs[grp % 2]
                for j in range(4):
                    k.act(ysqs[j][:, :nt], ZS[:, 4 * grp + j, cols], AF.Square)
                    k.mm(pn[:, :nt], onesb, ysqs[j][:, :nt], start=(j == 0), stop=(j == 3))
                k.act(yr[:, :nt], pn[:, :nt], AF.Ln, scale=1.0 / 512, bias=epsc[:, 0:1])
                k.act(yr[:, :nt], yr[:, :nt], AF.Exp, scale=-0.5)
                for j in range(4):
                    pr = 4 * grp + j
                    k.stt('dve', ZS[:, pr, cols], ZS[:, pr, cols], snorm[:, pr:pr + 1], yr[:, :nt], ALU.mult, ALU.mult)
            k.release(yidx)
        if tl['last_p']:
            k.dma('sp', o_sSp, sST)
        if DBG['stop'] == 'ssd_mix':
            dumpb(ZS, 16, T)
            dump(sST, 2048, 8192)
            return True
        k.barrier()
        out_proj(1, T, tl, d_swout, ZS)

    tiles = [dict(c0=512 * i, T=512, kind='p', last_p=(i == 3)) for i in range(4)]
    tiles.append(dict(c0=2048, T=64, kind='s', last_p=False))
    if DBG['stop']:
        tiles = [tiles[DBG.get('tile', 0)]]
    def _main(tiles):
        for ti, tl in enumerate(tiles):
            T = tl['T']
            c0 = tl['c0']
            rstate['blk'] = 0
            rstate['first'] = (ti == 0)
            k.dma('sp', X[:, :, :T], d_xT[:, :, c0:c0 + T])
            if gdn(T, tl):
                return k.finish(), hc
            if DBG['stop'] == 'gdn_out':
                dumpf(X, 8, T)
                return k.finish(), hc
            ffn(0, T, tl)
            if DBG['stop'] == 'l0':
                dumpf(X, 8, T)
                return k.finish(), hc
            if ssd(T, tl):
                return k.finish(), hc
            if DBG['stop'] == 'ssd_out':
                dumpf(X, 8, T)
                return k.finish(), hc
            ffn(1, T, tl)
            rstd = rms_rstd(lambda c: X[:, c, :T], 8, T, 1.0 / D)
            for c in range(8):
                k.stt('dve', X[:, c, :T], X[:, c, :T], nfin[:, c:c + 1], rstd[:, :T], ALU.mult, ALU.mult)
            k.dma('sp', o_yT[:, :, c0:c0 + T], X[:, :, :T])
            if ti == 0:
                k.barrier()
        k.dma('sp', o_ghalo, ghalo)
        k.dma('sp', o_shalo, shalo)
        return k.finish(), hc

    try:
        return _main(tiles)
    except _Stop:
        return k.finish(), hc


def prep_shared(inp):
    f = lambda a: np.ascontiguousarray(a, dtype=np.float32)
    sh = {}
    wm = inp['w_mod']
    sh['wmod'] = np.stack([blk_w(f(wm[l]), 256) for l in range(2)])
    sh['bmod'] = np.ascontiguousarray(np.stack([colvec(f(inp['b_mod'][l])) for l in range(2)], axis=1))
    sh['nmix'] = np.ascontiguousarray(np.stack([colvec(f(inp['norm_mix'][l])) for l in range(2)], axis=1))
    sh['nffn'] = np.ascontiguousarray(np.stack([colvec(f(inp['norm_ffn'][l])) for l in range(2)], axis=1))
    sh['nfin'] = colvec(f(inp['norm_final']))
    gw = f(inp['gdn_w_in'][0])
    sh['gwin'] = blk_w(gw[:, :6144], 256)
    sh['gwba'] = np.ascontiguousarray(gw[:, 6144:6176].reshape(8, 128, 32).transpose(1, 0, 2))
    sh['gcw'] = np.ascontiguousarray(f(inp['gdn_conv_w'][0]).reshape(4, 32, 128).transpose(2, 1, 0))
    sh['galog'] = np.ascontiguousarray(np.broadcast_to(f(inp['gdn_a_log'][0])[None, :], (128, 16)))
    sh['gdtb'] = np.ascontiguousarray(np.broadcast_to(f(inp['gdn_dt_bias'][0])[None, :], (128, 16)))
    sh['gnorm'] = f(inp['gdn_norm'][0]).reshape(128, 1)
    sh['gwout'] = blk_w(f(inp['gdn_w_out'][0]), 128)
    sw = f(inp['ssm_w_in'][0])
    sh['swin'] = blk_w(sw[:, :5120], 256)
    sh['swdt'] = np.ascontiguousarray(sw[:, 5120:5152].reshape(8, 128, 32).transpose(1, 0, 2))
    sh['scw'] = np.ascontiguousarray(f(inp['ssm_conv_w'][0]).reshape(4, 24, 128).transpose(2, 1, 0))
    sh['scb'] = colvec(f(inp['ssm_conv_b'][0]))
    sh['salog'] = np.ascontiguousarray(np.broadcast_to(f(inp['ssm_a_log'][0])[None, :], (128, 32)))
    sh['sdtb'] = np.ascontiguousarray(np.broadcast_to(f(inp['ssm_dt_bias'][0])[None, :], (128, 32)))
    sh['sd'] = colvec(np.repeat(f(inp['ssm_d'][0]), 64))
    sh['snorm'] = colvec(f(inp['ssm_norm'][0]))
    sh['swout'] = blk_w(f(inp['ssm_w_out'][0]), 128)
    fg = inp['ffn_w_gate_up']
    fgu = []
    for l in range(2):
        w = f(fg[l])
        gate, up = w[:, :FFH].reshape(1024, 22, 128), w[:, FFH:].reshape(1024, 22, 128)
        cat = np.concatenate([gate, up], axis=2).reshape(1024, 22 * 256)
        fgu.append(blk_w(cat, 256))
    sh['fgu'] = np.stack(fgu)
    sh['fdn'] = np.stack([blk_w(f(inp['ffn_w_down'][l]), 128) for l in range(2)])
    return sh


def prep_core(inp, i):
    f = lambda a: np.ascontiguousarray(a, dtype=np.float32)
    m = {}
    xs = inp['x_sample'][16 * i:16 * i + 16].reshape(64, D)
    xa = np.concatenate([inp['x_prompt'][i], xs], axis=0)
    m['xT'] = f(xa.T.reshape(8, 128, NTOK).transpose(1, 0, 2))
    ca = np.concatenate([inp['c_prompt'][i:i + 1], inp['c_sample'][16 * i:16 * i + 16]], axis=0)
    m['cT'] = f(ca.T.reshape(8, 128, NSEQ).transpose(1, 0, 2))
    gs = inp['state_gdn'][0, 16 * i:16 * i + 16]
    m['gS0'] = f(gs.transpose(1, 2, 0, 3))
    gh = np.zeros((128, 32, NSEQ, 3), np.float32)
    gc = inp['state_gdn_conv'][0, 16 * i:16 * i + 16]
    gh[:, :, 1:, :] = gc.reshape(16, 3, 32, 128).transpose(3, 2, 0, 1)
    m['ghalo'] = gh
    ss = inp['state_ssm'][0, 16 * i:16 * i + 16]
    m['sS0'] = f(ss.transpose(3, 0, 1, 2).reshape(128, 16, 2048))
    shl = np.zeros((128, 24, NSEQ, 3), np.float32)
    sc = inp['state_ssm_conv'][0, 16 * i:16 * i + 16]
    shl[:, :, 1:, :] = sc.reshape(16, 3, 24, 128).transpose(3, 2, 0, 1)
    m['shalo'] = shl
    return m


def kernel(**inp):
    inp = {k_: np.asarray(v) for k_, v in inp.items()}
    if 'nc' not in _NC_CACHE:
        _NC_CACHE['nc'] = build()
    nc, hc = _NC_CACHE['nc']
    sh = prep_shared(inp)
    in_maps = []
    for i in range(NCORES):
        m = dict(sh)
        m.update(prep_core(inp, i))
        for n, a in hc.items():
            m['c_' + n] = a
        in_maps.append(m)
    res = run_bass_kernel_spmd(nc, in_maps, core_ids=list(range(NCORES)))
    R = res.results
    y_p = np.zeros((8, 2048, D), np.float32)
    y_s = np.zeros((128, 4, D), np.float32)
    g_sp = np.zeros((1, 8, 16, 128, 128), np.float32)
    g_cp = np.zeros((1, 8, 3, 4096), np.float32)
    s_sp = np.zeros((1, 8, 32, 64, 128), np.float32)
    s_cp = np.zeros((1, 8, 3, 3072), np.float32)
    g_ss = np.zeros((1, 128, 16, 128, 128), np.float32)
    g_cs = np.zeros((1, 128, 3, 4096), np.float32)
    s_ss = np.zeros((1, 128, 32, 64, 128), np.float32)
    s_cs = np.zeros((1, 128, 3, 3072), np.float32)
    for i in range(NCORES):
        r = R[i]
        yT = np.asarray(r['yT']).transpose(1, 0, 2).reshape(D, NTOK).T
        y_p[i] = yT[:2048]
        y_s[16 * i:16 * i + 16] = yT[2048:].reshape(16, 4, D)
        g_sp[0, i] = np.asarray(r['gSp']).transpose(1, 0, 2)
        g_ss[0, 16 * i:16 * i + 16] = np.asarray(r['gSs']).transpose(2, 0, 1, 3)
        gh = np.asarray(r['ghalo_o'])
        gflat = gh.transpose(2, 3, 1, 0).reshape(NSEQ, 3, 4096)
        g_cp[0, i] = gflat[0]
        g_cs[0, 16 * i:16 * i + 16] = gflat[1:]
        s_sp[0, i] = np.asarray(r['sSp']).reshape(128, 32, 64).transpose(1, 2, 0)
        s_ss[0, 16 * i:16 * i + 16] = np.asarray(r['sSs']).reshape(128, 16, 32, 64).transpose(1, 2, 3, 0)
        shl = np.asarray(r['shalo_o'])
        sflat = shl.transpose(2, 3, 1, 0).reshape(NSEQ, 3, 3072)
        s_cp[0, i] = sflat[0]
        s_cs[0, 16 * i:16 * i + 16] = sflat[1:]
    return (y_p, y_s, g_sp, g_cp, s_sp, s_cp, g_ss, g_cs, s_ss, s_cs)
```
